# Optimizing a Trainium2 kernel written in Bass

```python
import math
import jax, jax.numpy as jnp
from jax import lax
import numpy as np


D_MODEL = 1024
BATCH = 32
SEQ = 2048
DEPTH = 1

CHUNK = 64
RET_HEADS = 4
RET_HEAD_DIM = 128
RET_WIDTH = RET_HEADS * RET_HEAD_DIM
ROPE_BASE = 10000.0
SG_BLOCK = 128
SG_GROUPS = 4
SG_DIM = 128
SG_WIDTH = SG_GROUPS * SG_DIM
MIX_WIDTH = RET_WIDTH + SG_WIDTH
IN_SPLITS = (RET_WIDTH, RET_WIDTH, RET_WIDTH, RET_WIDTH, SG_WIDTH, SG_WIDTH)
IN_WIDTH = sum(IN_SPLITS)
D_FF = 4 * D_MODEL
PLE_DIM = 256
EPS = 1e-6

kernel_name = "hybrid_retention_gmlp_streaming_layer"


def rmsnorm(x, g):
    xf = x.astype(jnp.float32)
    y = xf * lax.rsqrt(jnp.mean(xf * xf, axis=-1, keepdims=True) + EPS)
    return (y * g.astype(jnp.float32)).astype(x.dtype)


def layernorm_nogain(xf):
    mu = jnp.mean(xf, axis=-1, keepdims=True)
    xc = xf - mu
    return xc * lax.rsqrt(jnp.mean(xc * xc, axis=-1, keepdims=True) + EPS)


def rotary(x, pos):
    d = x.shape[-1]
    freqs = ROPE_BASE ** (-jnp.arange(0, d, 2, dtype=jnp.float32) / d)
    ang = pos[:, None] * freqs[None, :]
    cos = jnp.cos(ang)[None, :, None, :].astype(x.dtype)
    sin = jnp.sin(ang)[None, :, None, :].astype(x.dtype)
    x1, x2 = x[..., : d // 2], x[..., d // 2:]
    return jnp.concatenate([x1 * cos - x2 * sin, x2 * cos + x1 * sin], axis=-1)


def chunk_retention(q, k, v):
    B, S, H, D = q.shape
    N = S // CHUNK
    q, k, v = (a.astype(jnp.float32) for a in (q, k, v))
    log_g = jnp.log(1.0 - 2.0 ** (-5.0 - jnp.arange(H, dtype=jnp.float32)))
    t = jnp.arange(CHUNK, dtype=jnp.float32)
    intra_decay = jnp.exp(jnp.abs(t[:, None] - t[None, :])[None] * log_g[:, None, None])
    q_dec = jnp.exp((t[:, None] + 1.0) * log_g[None, :])
    k_dec = jnp.exp((CHUNK - 1.0 - t)[:, None] * log_g[None, :])
    c_dec = jnp.exp(CHUNK * log_g)

    qc = q.reshape(B, N, CHUNK, H, D)
    kc = k.reshape(B, N, CHUNK, H, D)
    vc = v.reshape(B, N, CHUNK, H, D)
    scores = jnp.einsum('bnihd,bnjhd->bnhij', qc, kc) * intra_decay
    o_intra = jnp.einsum('bnhij,bnjhe->bnihe', scores, vc)

    def step(state, inp):
        qn, kn, vn = inp
        o = jnp.einsum('bihd,bhde->bihe', qn * q_dec[None, :, :, None], state)
        state = state * c_dec[None, :, None, None] + jnp.einsum(
            'bjhd,bjhe->bhde', kn * k_dec[None, :, :, None], vn)
        return state, o

    xs = (qc.transpose(1, 0, 2, 3, 4), kc.transpose(1, 0, 2, 3, 4), vc.transpose(1, 0, 2, 3, 4))
    state0 = jnp.zeros((B, H, D, D), jnp.float32)
    _, o_cross = lax.scan(step, state0, xs)
    o = o_intra + o_cross.transpose(1, 0, 2, 3, 4)
    return o.reshape(B, S, H, D)


def spatial_gating(u, sv, w_s, b_s, sg_g):
    B, S, _ = sv.shape
    svn = (layernorm_nogain(sv.astype(jnp.float32)) * sg_g.astype(jnp.float32)).astype(sv.dtype)
    vb = svn.reshape(B, S // SG_BLOCK, SG_BLOCK, SG_GROUPS, SG_DIM)
    idx = jnp.arange(SG_BLOCK)
    allowed = (idx[None, :] // CHUNK) <= (idx[:, None] // CHUNK)
    w = jnp.where(allowed[None], w_s, jnp.zeros_like(w_s)).astype(sv.dtype)
    s = jnp.einsum('gij,bnjgc->bnigc', w, vb) + b_s.T.astype(sv.dtype)[None, None, :, :, None]
    return u * s.reshape(B, S, SG_WIDTH)


def setup_inputs(seed: int = 0) -> dict:
    key = jax.random.key(seed)
    ks = jax.random.split(key, 20)
    f32 = jnp.float32
    nrm = lambda k, shape, scale: jax.random.normal(k, shape, f32) * scale
    gain = lambda k, shape: 1.0 + 0.01 * jax.random.normal(k, shape, f32)
    return {
        "x": jax.random.normal(ks[0], (BATCH, SEQ, D_MODEL), f32),
        "p": jax.random.normal(ks[1], (DEPTH, BATCH, SEQ, PLE_DIM), f32),
        "g_mix": gain(ks[2], (DEPTH, D_MODEL)),
        "w_in": nrm(ks[3], (DEPTH, D_MODEL, IN_WIDTH), D_MODEL ** -0.5),
        "ret_norm_g": gain(ks[4], (DEPTH, RET_WIDTH)),
        "sg_norm_g": gain(ks[5], (DEPTH, SG_WIDTH)),
        "w_s": nrm(ks[6], (DEPTH, SG_GROUPS, SG_BLOCK, SG_BLOCK), SG_BLOCK ** -0.5),
        "b_s": gain(ks[7], (DEPTH, SG_GROUPS, SG_BLOCK)),
        "w_out": nrm(ks[8], (DEPTH, MIX_WIDTH, D_MODEL), MIX_WIDTH ** -0.5),
        "g_ffn": gain(ks[9], (DEPTH, D_MODEL)),
        "w_ff1": nrm(ks[10], (DEPTH, D_MODEL, D_FF), D_MODEL ** -0.5),
        "w_ff2": nrm(ks[11], (DEPTH, D_FF, D_MODEL), D_FF ** -0.5),
        "g_ple": gain(ks[12], (DEPTH, D_MODEL)),
        "w_ple_gate": nrm(ks[13], (DEPTH, D_MODEL, D_MODEL), D_MODEL ** -0.5),
        "w_ple": nrm(ks[14], (DEPTH, PLE_DIM, D_MODEL), PLE_DIM ** -0.5),
        "g_final": gain(ks[15], (D_MODEL,)),
    }


def reference(x, p, g_mix, w_in, ret_norm_g, sg_norm_g, w_s, b_s, w_out, g_ffn,
              w_ff1, w_ff2, g_ple, w_ple_gate, w_ple, g_final):
    B, S, _ = x.shape
    pos = jnp.arange(S, dtype=jnp.float32)
    cuts = list(np.cumsum(IN_SPLITS)[:-1])
    for i in range(DEPTH):
        h = rmsnorm(x, g_mix[i])
        z = h @ w_in[i]
        q, k, v, g, u, sv = jnp.split(z, cuts, axis=-1)
        q = rotary(q.reshape(B, S, RET_HEADS, RET_HEAD_DIM), pos)
        k = rotary(k.reshape(B, S, RET_HEADS, RET_HEAD_DIM), pos) * (RET_HEAD_DIM ** -0.5)
        v = v.reshape(B, S, RET_HEADS, RET_HEAD_DIM)
        r = chunk_retention(q, k, v)
        r = layernorm_nogain(r).reshape(B, S, RET_WIDTH) * ret_norm_g[i].astype(jnp.float32)
        y_ret = jax.nn.silu(g) * r.astype(x.dtype)
        y_sg = spatial_gating(jax.nn.gelu(u), jax.nn.gelu(sv), w_s[i], b_s[i], sg_norm_g[i])
        x = x + jnp.concatenate([y_ret, y_sg], axis=-1) @ w_out[i]
        hf = rmsnorm(x, g_ffn[i]) @ w_ff1[i]
        x = x + jnp.square(jax.nn.relu(hf)) @ w_ff2[i]
        gate = jax.nn.sigmoid(rmsnorm(x, g_ple[i]) @ w_ple_gate[i])
        x = x + gate * (p[i] @ w_ple[i])
    return rmsnorm(x, g_final)
```

```python
import contextlib
import numpy as np
import concourse.bass as bass
import concourse.mybir as mybir
from concourse.bass_utils import run_bass_kernel_spmd

F32 = mybir.dt.float32
BF16 = mybir.dt.bfloat16
I32 = mybir.dt.int32
AF = mybir.ActivationFunctionType
ALU = mybir.AluOpType

D = 1024
SEQ = 2048
TT = 512
NB = 4
EPS = 1e-6
NRING = 4
N_CORES = 8
SEQ_PER_CORE = 4


class Op:
    __slots__ = ("eng", "fn", "dma", "ndma", "key", "val", "marked", "waits", "clock", "rank")


class Sched:
    ENGS = ("pe", "act", "dve", "pool", "sp")

    def __init__(self):
        self.eng_ops = {e: [] for e in self.ENGS}
        self.comp_ops = {e: [] for e in self.ENGS}
        self.issue_clock = {e: {} for e in self.ENGS}
        self.last_writer = {}
        self.readers = {}
        self.dma_cum = {}
        self.n_ops = 0

    def add(self, eng, fn, reads=(), writes=(), dma=None, ndma=1):
        op = Op()
        op.eng, op.fn, op.dma, op.ndma = eng, fn, dma, ndma
        op.marked = False
        self.eng_ops[eng].append(op)
        self.n_ops += 1
        if dma is None:
            self.comp_ops[eng].append(op)
            op.key, op.val = eng, len(self.comp_ops[eng])
        else:
            cum = self.dma_cum.get(dma, 0) + 16 * ndma
            self.dma_cum[dma] = cum
            op.key, op.val = ("dma", dma), cum
        deps = []
        for b in reads:
            w = self.last_writer.get(b)
            if w is not None:
                deps.append((w, 0))
        for b in writes:
            w = self.last_writer.get(b)
            if w is not None:
                deps.append((w, 1))
            for r in self.readers.get(b, {}).values():
                deps.append((r, 2))
        clk = self.issue_clock[eng]
        waits = {}
        for d, kind in deps:
            if d is op:
                continue
            if d.dma is None and d.eng == eng and dma is None:
                if eng == "pe" or kind != 0:
                    continue
            if clk.get(d.key, 0) >= d.val:
                continue
            if waits.get(d.key, 0) < d.val:
                waits[d.key] = d.val
            d.marked = True
            for k, v in d.clock.items():
                if clk.get(k, 0) < v:
                    clk[k] = v
        op.waits = list(waits.items())
        op.clock = dict(clk)
        op.clock[op.key] = op.val
        for b in reads:
            self.readers.setdefault(b, {})[op.key] = op
        for b in writes:
            self.last_writer[b] = op
            self.readers[b] = {}
        return op

    def finalize(self):
        for e in self.ENGS:
            r = 0
            for op in self.comp_ops[e]:
                if op.marked:
                    r += 1
                op.rank = r


def _emit_engine(eng_name, eobj, sched, sems):
    comp = sched.comp_ops
    for op in sched.eng_ops[eng_name]:
        for key, val in op.waits:
            if isinstance(key, tuple):
                eobj.wait_ge(sems[key], val)
            else:
                eobj.wait_ge(sems[key], comp[key][val - 1].rank)
        ins = op.fn(eobj)
        if op.dma is not None:
            if not isinstance(ins, (list, tuple)):
                ins = [ins]
            assert len(ins) == op.ndma
            for i_ in ins:
                i_.then_inc(sems[op.key], 16)
        elif op.marked:
            ins.then_inc(sems[op.key], 1)


def _host_consts():
    f32 = np.float32
    H = 4
    gam = (1.0 - 2.0 ** (-5.0 - np.arange(H, dtype=np.float64)))
    lg = np.log(gam)
    t = np.arange(128, dtype=np.float64)
    ci = (t // 64)[:, None]
    i_idx = t[None, :]
    j_idx = t[:, None]
    cj = (t // 64)[:, None]
    cii = (t // 64)[None, :]
    s = 128.0 ** -0.5
    DTm = np.zeros((128, H, 128), np.float64)
    for h in range(H):
        same = (cj == cii)
        earlier = (cj < cii)
        w = np.where(same, np.exp(np.abs(i_idx - j_idx) * lg[h]),
                     np.where(earlier, np.exp((i_idx - j_idx) * lg[h]), 0.0))
        qdec = np.exp((i_idx + 1.0) * lg[h])
        DTm[:, h, :] = s * w / qdec
    KD = np.stack([s * np.exp((127.0 - t) * lg[h]) for h in range(H)], axis=1)
    EPSQ = np.stack([EPS / np.exp(2.0 * (t + 1.0) * lg[h]) for h in range(H)], axis=1)
    MASKT = (cj <= cii).astype(np.float64)
    freqs = (10000.0 ** (-np.arange(0, 128, 2, dtype=f32) / f32(128))).astype(f32)
    pos = np.arange(SEQ, dtype=f32)
    ang = (pos[:, None] * freqs[None, :]).astype(f32)
    cos = np.cos(ang).astype(f32).T
    sin = np.sin(ang).astype(f32).T
    CC = np.concatenate([cos, cos], axis=0)
    SW = np.concatenate([sin, -sin], axis=0)
    ROT = np.zeros((SEQ // TT, 128, 2, TT), f32)
    for tp in range(SEQ // TT):
        ROT[tp, :, 0, :] = CC[:, tp * TT:(tp + 1) * TT]
        ROT[tp, :, 1, :] = SW[:, tp * TT:(tp + 1) * TT]
    return dict(
        DTm=DTm.reshape(128, 512).astype(f32), KD=KD.astype(f32), EPSQ=EPSQ.astype(f32),
        MASKT=MASKT.astype(f32), ROT=ROT,
        IDENT=np.eye(128, dtype=f32),
    )


def build(nseq):
    NT = nseq * (SEQ // TT)
    NTOK = NT * TT
    nc = bass.Bass("TRN2", target_bir_lowering=False)

    def din(name, shape, dt=F32):
        return nc.dram_tensor(name, list(shape), dt, kind="ExternalInput").ap()

    x_d = din("x", [NTOK, D])
    p_d = din("p", [NTOK, 256])
    w_in_d = din("w_in", [D, 3072])
    w_out_d = din("w_out", [D, D])
    w1_d = din("w_ff1", [D, 4096])
    w2_d = din("w_ff2", [4096, D])
    wg_d = din("w_gate", [D, D])
    wple_d = din("w_ple", [256, D])
    gT_d = din("gT", [128, 24])
    gfin_d = din("gfin_b", [128, D])
    retg_d = din("retg_b", [128, 512])
    sgg_d = din("sgg_b", [128, 512])
    bsT_d = din("bsT", [128, 4])
    wsT_d = din("wsT", [128, 512])
    DTm_d = din("DTm", [128, 512])
    KD_d = din("KD", [128, 4])
    EPSQ_d = din("EPSQ", [128, 4])
    MASKT_d = din("MASKT", [128, 128])
    ROT_d = din("ROT", [SEQ // TT, 128, 2, TT])
    IDENT_d = din("IDENT", [128, 128])
    out_d = nc.dram_tensor("out", [NTOK, D], F32, kind="ExternalOutput").ap()
    wqkb_d = nc.dram_tensor("wqkb", [2, D, 512], BF16, kind="Internal").ap()
    w1b_d = nc.dram_tensor("w1b", [8, D, 512], BF16, kind="Internal").ap()
    w2b_d = nc.dram_tensor("w2b", [8, D, 512], BF16, kind="Internal").ap()

    S = Sched()
    es = contextlib.ExitStack()

    def sb(name, shape, dt):
        return es.enter_context(nc.sbuf_tensor(name, list(shape), dt))

    with es:
        win = sb("win", [128, 8, 2048], BF16)
        wout = sb("wout", [128, 8, 1024], BF16)
        wgs = sb("wgs", [128, 8, 1024], BF16)
        wple = sb("wple", [128, 2, 1024], BF16)
        ring = sb("ring", [128, NRING, 8, 512], BF16)
        xb = sb("xb", [128, NB, D], F32)
        actT = sb("actT", [128, 8, TT], BF16)
        big = sb("big", [128, 16384], BF16)
        rot = sb("rot", [128, 2, TT], F32)
        pbf = sb("pbf", [128, NB, 256], BF16)
        pT = sb("pT", [128, 2, TT], BF16)
        scr = sb("scr", [128, 4, 1024], F32)
        xsb = sb("xsb", [128, NB, D], BF16)
        Sst = sb("Sst", [128, 512], F32)
        Sbf = sb("Sbf", [128, 2, 512], BF16)
        ident = sb("ident", [128, 128], BF16)
        gT = sb("gT_s", [128, 24], F32)
        gfin = sb("gfin_s", [128, D], F32)
        retg = sb("retg_s", [128, 512], F32)
        sgg = sb("sgg_s", [128, 512], F32)
        bsT = sb("bsT_s", [128, 4], F32)
        wsTf = sb("wsTf", [128, 512], F32)
        wsTb = sb("wsTb", [128, 512], BF16)
        DTm = sb("DTm_s", [128, 512], F32)
        KD = sb("KD_s", [128, 4], F32)
        EPSQ = sb("EPSQ_s", [128, 4], F32)
        MASKT = sb("MASKT_s", [128, 128], F32)
        stats = sb("stats", [128, 96], F32)
        st6 = sb("st6", [128, 2, 4, 6], F32)
        mv4 = sb("mv4", [128, 2, 4, 2], F32)
        psA = es.enter_context(nc.psum_tensor("psA", [128, 6, 512], F32))
        psT = es.enter_context(nc.psum_tensor("psT", [128, 2, 1024], BF16))

        hT = big[:, :].rearrange("p (f n) -> p f n", n=TT)
        qk = big[:, 0:4096].rearrange("p (c n) -> p c n", n=TT)
        OV = 4096

        def ov_bf(off, n):
            return big[:, OV + off: OV + off + n]

        def ov_f32(off, n):
            return big[:, OV + off: OV + off + 2 * n].bitcast(F32)

        PBSZ = 5120
        v_bf = [ov_bf(pb * PBSZ + 0, 512) for pb in range(2)]
        svn_bf = [ov_bf(pb * PBSZ + 512, 512) for pb in range(2)]
        ktok = [ov_bf(pb * PBSZ + 1024, 512) for pb in range(2)]
        PT = [ov_bf(pb * PBSZ + 1536, 512) for pb in range(2)]
        ybf = [ov_bf(pb * PBSZ + 2048, 1024) for pb in range(2)]
        gsb = [ov_f32(pb * PBSZ + 3072, 512) for pb in range(2)]
        ugb = [ov_f32(pb * PBSZ + 4096, 512) for pb in range(2)]
        assert OV + 2 * PBSZ <= 16384

        state = {"bank": 0, "tb": 0, "scr": 0, "unit": 0}

        def bankA():
            b = state["bank"]
            state["bank"] = (b + 1) % 6
            return b

        def bankT():
            b = state["tb"]
            state["tb"] = (b + 1) % 2
            return b

        def scr_slot():
            s_ = state["scr"]
            state["scr"] = (s_ + 1) % 4
            return s_

        def ld(eng, dst, src, key, wkey):
            S.add(eng, lambda e, dst=dst, src=src: e.dma_start(out=dst, in_=src), writes=[wkey], dma=key)

        ld("sp", gT[:, :], gT_d[:, :], "c_gT", "gT")
        ld("sp", DTm[:, :], DTm_d[:, :], "c_DTm", "DTm")
        ld("sp", KD[:, :], KD_d[:, :], "c_KD", "KD")
        ld("sp", EPSQ[:, :], EPSQ_d[:, :], "c_EPSQ", "EPSQ")
        ld("sp", MASKT[:, :], MASKT_d[:, :], "c_MASKT", "MASKT")
        ld("sp", wsTf[:, :], wsT_d[:, :], "c_wsT", "wsTf")
        ld("sp", bsT[:, :], bsT_d[:, :], "c_bsT", "bsT")
        ld("sp", retg[:, :], retg_d[:, :], "c_retg", "retg")
        ld("sp", sgg[:, :], sgg_d[:, :], "c_sgg", "sgg")
        ld("sp", gfin[:, :], gfin_d[:, :], "c_gfin", "gfin")
        ld("pool", ident[:, :], IDENT_d[:, :], "c_ident", "ident")
        S.add("dve", lambda e: e.tensor_tensor(
            out=wsTb[:, :].rearrange("p (g i) -> p g i", g=4),
            in0=wsTf[:, :].rearrange("p (g i) -> p g i", g=4),
            in1=MASKT[:, None, :].to_broadcast([128, 4, 128]), op=ALU.mult),
            reads=["wsTf", "MASKT"], writes=["wsTb"])

        units = []
        for t in range(NT):
            units += [("qk", 0), ("qk", 1)]
            units += [("w1", u) for u in range(8)]
            units += [("w2", u) for u in range(8)]
        unit_src = {"qk": wqkb_d, "w1": w1b_d, "w2": w2b_d}

        def issue_unit(n):
            if n >= len(units):
                return
            kind, u = units[n]
            slot = n % NRING
            src = unit_src[kind][u].rearrange("(kc p) n -> p kc n", p=128)
            S.add("sp", lambda e, slot=slot, src=src: e.dma_start(out=ring[:, slot, :, :], in_=src),
                  reads=[(kind + "b", u)], writes=[("ring", slot)], dma=("ring", slot))

        def next_unit():
            n = state["unit"]
            state["unit"] = n + 1
            issue_unit(n + NRING - 1)
            return n % NRING

        def cast_scratch(dst, src, key, wkey):
            S.add("pool", lambda e, dst=dst, src=src: e.dma_start(out=dst, in_=src), writes=[wkey], dma=key)

        for u in range(2):
            cast_scratch(wqkb_d[u], w_in_d[:, u * 512:(u + 1) * 512], "cs_qk%d" % u, ("qkb", u))
        S.add("pool", lambda e: e.dma_start(
            out=win[:, :, :], in_=w_in_d[:, 1024:3072].rearrange("(kc p) n -> p kc n", p=128)),
            writes=["win"], dma="cs_win")
        S.add("pool", lambda e: e.dma_start(
            out=wout[:, :, :], in_=w_out_d.rearrange("(kc p) n -> p kc n", p=128)),
            writes=["wout"], dma="cs_wout")
        for u in range(8):
            cast_scratch(w1b_d[u], w1_d[:, u * 512:(u + 1) * 512], "cs_w1_%d" % u, ("w1b", u))
        for u in range(8):
            dh, fg = u // 4, u % 4
            cast_scratch(w2b_d[u], w2_d[fg * 1024:(fg + 1) * 1024, dh * 512:(dh + 1) * 512],
                         "cs_w2_%d" % u, ("w2b", u))
        S.add("pool", lambda e: e.dma_start(
            out=wgs[:, :, :], in_=wg_d.rearrange("(kc p) n -> p kc n", p=128)),
            writes=["wgs"], dma="cs_wg")
        S.add("pool", lambda e: e.dma_start(
            out=wple[:, :, :], in_=wple_d.rearrange("(kc p) n -> p kc n", p=128)),
            writes=["wple"], dma="cs_wple")

        def load_x(t, b):
            r0 = t * TT + b * 128
            S.add("sp", lambda e, r0=r0, b=b: e.dma_start(out=xb[:, b, :], in_=x_d[r0:r0 + 128, :]),
                  writes=[("x", b)], dma=("xl", b))

        def load_p(t):
            r0 = t * TT
            S.add("pool", lambda e, r0=r0: e.dma_start(
                out=pbf[:, :, :], in_=p_d[r0:r0 + TT, :].rearrange("(b p) c -> p b c", p=128)),
                writes=["pbf"], dma="pl")

        def load_rot(t):
            tp = t % (SEQ // TT)
            S.add("sp", lambda e, tp=tp: e.dma_start(out=rot[:, :, :], in_=ROT_d[tp]),
                  writes=["rot"], dma="rl")

        for b in range(NB):
            load_x(0, b)
        load_rot(0)
        load_p(0)
        for n in range(NRING - 1):
            issue_unit(n)
        PRO = {"pending": True}

        MAGIC = 0x5F3759DF
        CDEC_H = [float(np.exp(128.0 * np.log(1.0 - 2.0 ** (-5.0 - h)))) for h in range(4)]

        def rsqrt_dve(a_ap, y_ap, t_ap, k, keys):
            ai = a_ap.bitcast(I32)
            yi = y_ap.bitcast(I32)
            S.add("dve", lambda e: e.tensor_single_scalar(out=yi, in_=ai, scalar=1, op=ALU.arith_shift_right),
                  reads=keys, writes=keys)
            S.add("dve", lambda e: e.tensor_scalar(out=yi, in0=yi, scalar1=-1, scalar2=MAGIC,
                                                   op0=ALU.mult, op1=ALU.add), reads=keys, writes=keys)
            for _ in range(2):
                if k == 1:
                    S.add("dve", lambda e: e.scalar_tensor_tensor(out=t_ap, in0=y_ap, scalar=a_ap, in1=y_ap,
                                                                  op0=ALU.mult, op1=ALU.mult),
                          reads=keys, writes=keys)
                else:
                    S.add("dve", lambda e: e.tensor_tensor(out=t_ap, in0=y_ap, in1=y_ap, op=ALU.mult),
                          reads=keys, writes=keys)
                    S.add("dve", lambda e: e.tensor_tensor(out=t_ap, in0=t_ap, in1=a_ap, op=ALU.mult),
                          reads=keys, writes=keys)
                S.add("dve", lambda e: e.tensor_scalar(out=t_ap, in0=t_ap, scalar1=-0.5, scalar2=1.5,
                                                       op0=ALU.mult, op1=ALU.add), reads=keys, writes=keys)
                S.add("dve", lambda e: e.tensor_tensor(out=y_ap, in0=y_ap, in1=t_ap, op=ALU.mult),
                      reads=keys, writes=keys)

        def rstd_from_ss(ss_ap, rs_ap, a_ap, t_ap, k, n, rkeys, wkeys):
            S.add("dve", lambda e: e.tensor_scalar(out=a_ap, in0=ss_ap, scalar1=1.0 / n, scalar2=EPS,
                                                   op0=ALU.mult, op1=ALU.add), reads=rkeys, writes=wkeys)
            rsqrt_dve(a_ap, rs_ap, t_ap, k, wkeys)

        def norm_sq(b):
            sl = scr_slot()
            junk = scr[:, sl, :].bitcast(BF16)[:, 0:1024]
            S.add("act", lambda e: e.activation(out=junk, in_=xb[:, b, :], func=AF.Square,
                                                accum_out=stats[:, b:b + 1]),
                  reads=[("x", b)], writes=[("scr", sl), ("ss", b)])

        def norm_rstd(b):
            rstd_from_ss(stats[:, b:b + 1], stats[:, 4 + b:5 + b], stats[:, 32 + b:33 + b],
                         stats[:, 36 + b:37 + b], 1, float(D), [("ss", b)], [("rs", b)])

        def norm_scale(b):
            S.add("act", lambda e: e.activation(out=xsb[:, b, :], in_=xb[:, b, :], func=AF.Identity,
                                                scale=stats[:, 4 + b:5 + b]),
                  reads=[("x", b), ("rs", b)], writes=[("xs", b)])

        def norm_elem(b):
            norm_sq(b)
            norm_rstd(b)
            norm_scale(b)

        def norm_transpose(b, gcol):
            tb = bankT()
            for kc in range(8):
                S.add("pe", lambda e, kc=kc: e.transpose(out=psT[:, tb, kc * 128:(kc + 1) * 128],
                                                         in_=xsb[:, b, kc * 128:(kc + 1) * 128],
                                                         identity=ident[:, :]),
                      reads=[("xs", b), "ident"], writes=[("psT", tb)])
            S.add("dve", lambda e: e.tensor_tensor(
                out=actT[:, :, b * 128:(b + 1) * 128],
                in0=psT[:, tb, :].rearrange("p (k n) -> p k n", k=8),
                in1=gT[:, gcol:gcol + 8, None].to_broadcast([128, 8, 128]), op=ALU.mult),
                reads=[("psT", tb), "gT"], writes=[("actT", b)])

        ALL_ACT = [("actT", b) for b in range(NB)]

        for t in range(NT):
            tp = t % (SEQ // TT)

            if PRO["pending"]:
                for b in range(NB):
                    norm_elem(b)
                PRO["pending"] = False
            def phase_B():
                for qk_i in range(2):
                    slot = next_unit()
                    for c4 in range(4):
                        c = qk_i * 4 + c4
                        bk = bankA()
                        for kc in range(8):
                            S.add("pe", lambda e, kc=kc, c4=c4, slot=slot, bk=bk: e.matmul(
                                psA[:, bk, :], ring[:, slot, kc, c4 * 128:(c4 + 1) * 128], actT[:, kc, :],
                                start=(kc == 0), stop=(kc == 7)),
                                reads=[("ring", slot)] + ALL_ACT, writes=[("psA", bk)])
                        sl = scr_slot()
                        m1 = scr[:, sl, 0:512]
                        m2 = scr[:, sl, 512:1024]
                        S.add("dve", lambda e, bk=bk, m1=m1: e.tensor_tensor(
                            out=m1, in0=psA[:, bk, :], in1=rot[:, 0, :], op=ALU.mult),
                            reads=[("psA", bk), "rot"], writes=[("scr", sl)])
                        S.add("dve", lambda e, bk=bk, m2=m2: e.tensor_tensor(
                            out=m2[0:64, :], in0=psA[64:128, bk, :], in1=rot[64:128, 1, :], op=ALU.mult),
                            reads=[("psA", bk), "rot"], writes=[("scr", sl)])
                        S.add("dve", lambda e, bk=bk, m2=m2: e.tensor_tensor(
                            out=m2[64:128, :], in0=psA[0:64, bk, :], in1=rot[0:64, 1, :], op=ALU.mult),
                            reads=[("psA", bk), "rot"], writes=[("scr", sl)])
                        S.add("pool", lambda e, c=c, m1=m1, m2=m2: e.tensor_tensor(
                            out=qk[:, c, :], in0=m1, in1=m2, op=ALU.add),
                            reads=[("scr", sl)], writes=[("qk", c)])
                if t + 1 < NT:
                    load_rot(t + 1)

            def stage_X1(b):
                pb = b % 2
                cols = slice(b * 128, (b + 1) * 128)
                banks = []
                for gi in range(4):
                    bk = bankA()
                    banks.append(bk)
                    for kc in range(8):
                        S.add("pe", lambda e, kc=kc, gi=gi, bk=bk: e.matmul(
                            psA[:, bk, :], actT[:, kc, cols], win[:, kc, gi * 512:(gi + 1) * 512],
                            start=(kc == 0), stop=(kc == 7)),
                            reads=[("actT", b), "win"], writes=[("psA", bk)])
                sl = scr_slot()
                svg = scr[:, sl, 0:512]
                S.add("act", lambda e: e.activation(out=v_bf[pb], in_=psA[:, banks[0], :], func=AF.Copy),
                      reads=[("psA", banks[0])], writes=[("v", pb)])

                def do_silu():
                    S.add("act", lambda e: e.activation(out=gsb[pb], in_=psA[:, banks[1], :], func=AF.Silu),
                          reads=[("psA", banks[1])], writes=[("gs", pb)])

                def do_gelu():
                    S.add("act", lambda e: e.activation(out=svg, in_=psA[:, banks[3], :],
                                                        func=AF.Gelu_apprx_tanh),
                          reads=[("psA", banks[3])], writes=[("scr", sl)])
                    S.add("act", lambda e: e.activation(out=ugb[pb], in_=psA[:, banks[2], :],
                                                        func=AF.Gelu_apprx_tanh),
                          reads=[("psA", banks[2])], writes=[("ug", pb)])
                if b % 2 == 0:
                    do_silu(); do_gelu()
                else:
                    do_gelu(); do_silu()
                S.add("pool", lambda e: e.tensor_tensor(out=gsb[pb], in0=gsb[pb], in1=retg[:, :], op=ALU.mult),
                      reads=[("gs", pb), "retg"], writes=[("gs", pb)])
                s6 = st6[:, pb, 0, :]
                mv = mv4[:, pb, 0, :]
                S.add("dve", lambda e: e.bn_stats(out=s6, in_=svg),
                      reads=[("scr", sl)], writes=[("st6", pb)])
                S.add("dve", lambda e: e.bn_aggr(out=mv, in_=s6), reads=[("st6", pb)], writes=[("mv", pb)])
                rs1 = stats[:, 16 + pb:17 + pb]
                a1 = stats[:, 48 + pb:49 + pb]
                S.add("dve", lambda e: e.tensor_scalar(out=a1, in0=mv4[:, pb, 0, 1:2], scalar1=EPS, scalar2=None,
                                                       op0=ALU.add),
                      reads=[("mv", pb)], writes=[("rs1", pb)])
                rsqrt_dve(a1, rs1, stats[:, 50 + pb:51 + pb], 1, [("rs1", pb)])
                S.add("dve", lambda e: e.tensor_scalar(out=svg, in0=svg, scalar1=mv4[:, pb, 0, 0:1], scalar2=rs1,
                                                       op0=ALU.subtract, op1=ALU.mult),
                      reads=[("scr", sl), ("mv", pb), ("rs1", pb)], writes=[("scr", sl)])
                S.add("pool", lambda e: e.tensor_tensor(out=svn_bf[pb], in0=svg, in1=sgg[:, :], op=ALU.mult),
                      reads=[("scr", sl), "sgg"], writes=[("svn", pb)])

            def stage_X2(b):
                pb = b % 2
                cols = slice(b * 128, (b + 1) * 128)
                last = (tp == SEQ // TT - 1 and b == NB - 1)
                if not last:
                    tb = bankT()
                    for h in range(4):
                        S.add("pe", lambda e, h=h: e.transpose(out=psT[:, tb, h * 128:(h + 1) * 128],
                                                               in_=qk[:, 4 + h, cols], identity=ident[:, :]),
                              reads=[("qk", 4 + h), "ident"], writes=[("psT", tb)])
                    S.add("dve", lambda e: e.tensor_tensor(
                        out=ktok[pb].rearrange("p (h d) -> p h d", h=4),
                        in0=psT[:, tb, 0:512].rearrange("p (h d) -> p h d", h=4),
                        in1=KD[:, :, None].to_broadcast([128, 4, 128]), op=ALU.mult),
                        reads=[("psT", tb), "KD"], writes=[("ktok", pb)])
                bk = bankA()
                for h in range(4):
                    S.add("pe", lambda e, h=h: e.matmul(psA[:, bk, h * 128:(h + 1) * 128],
                                                        qk[:, 4 + h, cols], qk[:, h, cols],
                                                        start=True, stop=True),
                          reads=[("qk", 4 + h), ("qk", h)], writes=[("psA", bk)])
                S.add("dve", lambda e: e.tensor_tensor(out=PT[pb], in0=psA[:, bk, :], in1=DTm[:, :], op=ALU.mult),
                      reads=[("psA", bk), "DTm"], writes=[("PT", pb)])

            def stage_X(b):
                stage_X1(b)
                stage_X2(b)

            def stage_Y(b):
                pb = b % 2
                cols = slice(b * 128, (b + 1) * 128)
                first = (tp == 0 and b == 0)
                last = (tp == SEQ // TT - 1 and b == NB - 1)
                gb = (t * NB + b)
                ps_in = gb % 2
                bo = bankA()
                for h in range(4):
                    hs = slice(h * 128, (h + 1) * 128)
                    S.add("pe", lambda e, hs=hs: e.matmul(psA[:, bo, hs], PT[pb][:, hs], v_bf[pb][:, hs],
                                                          start=True, stop=first),
                          reads=[("PT", pb), ("v", pb)], writes=[("psA", bo)])
                    if not first:
                        S.add("pe", lambda e, hs=hs, h=h: e.matmul(psA[:, bo, hs], qk[:, h, cols],
                                                                    Sbf[:, ps_in, hs], start=False, stop=True),
                              reads=[("qk", h), ("Sbf", ps_in)], writes=[("psA", bo)])
                if not last:
                    bu = bankA()
                    for h in range(4):
                        hs = slice(h * 128, (h + 1) * 128)
                        S.add("pe", lambda e, hs=hs: e.matmul(psA[:, bu, hs], ktok[pb][:, hs], v_bf[pb][:, hs],
                                                              start=True, stop=True),
                              reads=[("ktok", pb), ("v", pb)], writes=[("psA", bu)])
                    if first:
                        S.add("dve", lambda e: e.tensor_copy(out=Sst[:, :], in_=psA[:, bu, :]),
                              reads=[("psA", bu)], writes=["Sst"])
                    else:
                        for h in range(4):
                            hs = slice(h * 128, (h + 1) * 128)
                            S.add("dve", lambda e, hs=hs, h=h: e.scalar_tensor_tensor(
                                out=Sst[:, hs], in0=Sst[:, hs], scalar=float(CDEC_H[h]), in1=psA[:, bu, hs],
                                op0=ALU.mult, op1=ALU.add),
                                reads=["Sst", ("psA", bu)], writes=["Sst"])
                    S.add("act", lambda e: e.activation(out=Sbf[:, 1 - ps_in, :], in_=Sst[:, :], func=AF.Copy),
                          reads=["Sst"], writes=[("Sbf", 1 - ps_in)])
                for h in range(4):
                    S.add("dve", lambda e, h=h: e.bn_stats(out=st6[:, pb, h, :], in_=psA[:, bo, h * 128:(h + 1) * 128]),
                          reads=[("psA", bo)], writes=[("st6", pb)])
                for h in range(4):
                    S.add("dve", lambda e, h=h: e.bn_aggr(out=mv4[:, pb, h, :], in_=st6[:, pb, h, :]),
                          reads=[("st6", pb)], writes=[("mv", pb)])
                rs4 = stats[:, 20 + 4 * pb:24 + 4 * pb]
                a4 = stats[:, 64 + 4 * pb:68 + 4 * pb]
                S.add("dve", lambda e: e.tensor_tensor(out=a4, in0=mv4[:, pb, :, 1], in1=EPSQ[:, :], op=ALU.add),
                      reads=[("mv", pb), "EPSQ"], writes=[("rs4", pb)])
                rsqrt_dve(a4, rs4, stats[:, 72 + 4 * pb:76 + 4 * pb], 4, [("rs4", pb)])
                sl = scr_slot()
                rn = scr[:, sl, 0:512]
                for h in range(4):
                    hs = slice(h * 128, (h + 1) * 128)
                    S.add("dve", lambda e, hs=hs, h=h: e.tensor_scalar(
                        out=rn[:, hs], in0=psA[:, bo, hs], scalar1=mv4[:, pb, h, 0:1],
                        scalar2=stats[:, 20 + 4 * pb + h:21 + 4 * pb + h],
                        op0=ALU.subtract, op1=ALU.mult),
                        reads=[("psA", bo), ("mv", pb), ("rs4", pb)], writes=[("scr", sl)])
                S.add("pool", lambda e: e.tensor_tensor(out=ybf[pb][:, 0:512], in0=rn, in1=gsb[pb], op=ALU.mult),
                      reads=[("scr", sl), ("gs", pb)], writes=[("y", pb)])
                bs_ = bankA()
                for g in range(4):
                    hs = slice(g * 128, (g + 1) * 128)
                    S.add("pe", lambda e, hs=hs: e.matmul(psA[:, bs_, hs], wsTb[:, hs], svn_bf[pb][:, hs],
                                                          start=True, stop=True),
                          reads=["wsTb", ("svn", pb)], writes=[("psA", bs_)])
                tg = scr[:, sl, 512:1024]
                S.add("dve", lambda e: e.tensor_tensor(
                    out=tg.rearrange("p (g c) -> p g c", g=4),
                    in0=psA[:, bs_, :].rearrange("p (g c) -> p g c", g=4),
                    in1=bsT[:, :, None].to_broadcast([128, 4, 128]), op=ALU.add),
                    reads=[("psA", bs_), "bsT"], writes=[("scr", sl)])
                S.add("pool", lambda e: e.tensor_tensor(out=ybf[pb][:, 512:1024], in0=tg, in1=ugb[pb], op=ALU.mult),
                      reads=[("scr", sl), ("ug", pb)], writes=[("y", pb)])

            def stage_Z(b):
                pb = b % 2
                tb = bankT()
                for kc in range(8):
                    S.add("pe", lambda e, kc=kc: e.transpose(out=psT[:, tb, kc * 128:(kc + 1) * 128],
                                                             in_=ybf[pb][:, kc * 128:(kc + 1) * 128],
                                                             identity=ident[:, :]),
                          reads=[("y", pb), "ident"], writes=[("psT", tb)])
                S.add("act", lambda e: e.activation(out=actT[:, :, b * 128:(b + 1) * 128],
                                                    in_=psT[:, tb, :].rearrange("p (k n) -> p k n", k=8),
                                                    func=AF.Copy),
                      reads=[("psT", tb)], writes=[("actT", b)])

            def stage_W(b):
                cols = slice(b * 128, (b + 1) * 128)
                for half in range(2):
                    bk = bankA()
                    hsl = slice(half * 512, (half + 1) * 512)
                    for kc in range(8):
                        S.add("pe", lambda e, kc=kc, bk=bk, hsl=hsl: e.matmul(
                            psA[:, bk, :], actT[:, kc, cols], wout[:, kc, hsl],
                            start=(kc == 0), stop=(kc == 7)),
                            reads=[("actT", b), "wout"], writes=[("psA", bk)])
                    S.add("dve", lambda e, bk=bk, hsl=hsl: e.tensor_tensor(
                        out=xb[:, b, hsl], in0=xb[:, b, hsl], in1=psA[:, bk, :], op=ALU.add),
                        reads=[("x", b), ("psA", bk)], writes=[("x", b)])
                norm_elem(b)

            def stage_DT(b):
                norm_transpose(b, 8)

            norm_transpose(0, 0)
            norm_transpose(1, 0)
            stage_X1(0)
            stage_X1(1)
            norm_transpose(2, 0)
            norm_transpose(3, 0)
            phase_B()
            order = [("X2", 0), ("X2", 1), ("Y", 0), ("X", 2), ("Y", 1), ("Z", 0), ("X", 3), ("Y", 2), ("Z", 1),
                     ("W", 0), ("Y", 3), ("Z", 2), ("W", 1), ("DT", 0), ("Z", 3), ("W", 2), ("DT", 1), ("W", 3),
                     ("DT", 2), ("DT", 3)]
            fns = {"X": stage_X, "X2": stage_X2, "Y": stage_Y, "Z": stage_Z, "W": stage_W, "DT": stage_DT}
            for nm, b in order:
                fns[nm](b)

            for u in range(8):
                slot = next_unit()
                for f4 in range(4):
                    f = u * 4 + f4
                    bk = bankA()
                    for kc in range(8):
                        S.add("pe", lambda e, kc=kc, f4=f4, slot=slot, bk=bk: e.matmul(
                            psA[:, bk, :], ring[:, slot, kc, f4 * 128:(f4 + 1) * 128], actT[:, kc, :],
                            start=(kc == 0), stop=(kc == 7)),
                            reads=[("ring", slot)] + ALL_ACT, writes=[("psA", bk)])
                    sl = scr_slot()
                    r_ = scr[:, sl, 0:512]
                    S.add("act", lambda e, bk=bk, r_=r_: e.activation(out=r_, in_=psA[:, bk, :], func=AF.Relu),
                          reads=[("psA", bk)], writes=[("scr", sl)])
                    S.add("dve" if f % 2 == 0 else "pool",
                          lambda e, f=f, r_=r_: e.tensor_tensor(out=hT[:, f, :], in0=r_, in1=r_, op=ALU.mult),
                          reads=[("scr", sl)], writes=[("hT", f)])
                if u == 2:
                    for b in range(NB):
                        tb = bankT()
                        for kc in range(2):
                            S.add("pe", lambda e, kc=kc, b=b, tb=tb: e.transpose(
                                out=psT[:, tb, kc * 128:(kc + 1) * 128],
                                in_=pbf[:, b, kc * 128:(kc + 1) * 128], identity=ident[:, :]),
                                reads=["pbf", "ident"], writes=[("psT", tb)])
                        S.add("act", lambda e, b=b, tb=tb: e.activation(
                            out=pT[:, :, b * 128:(b + 1) * 128],
                            in_=psT[:, tb, 0:256].rearrange("p (k n) -> p k n", k=2), func=AF.Copy),
                            reads=[("psT", tb)], writes=[("pT", b)])
                    if t + 1 < NT:
                        load_p(t + 1)

            for dh in range(2):
                hsl = slice(dh * 512, (dh + 1) * 512)
                fb = [bankA() for _ in range(NB)]
                for fg in range(4):
                    slot = next_unit()
                    for b in range(NB):
                        cols = slice(b * 128, (b + 1) * 128)
                        for fc in range(8):
                            f = fg * 8 + fc
                            S.add("pe", lambda e, f=f, fc=fc, slot=slot, fbk=fb[b], cols=cols: e.matmul(
                                psA[:, fbk, :], hT[:, f, cols], ring[:, slot, fc, :],
                                start=(f == 0), stop=(f == 31)),
                                reads=[("hT", f), ("ring", slot)], writes=[("psA", fb[b])])
                for b in range(NB):
                    S.add("dve", lambda e, b=b, hsl=hsl, fbk=fb[b]: e.tensor_tensor(
                        out=xb[:, b, hsl], in0=xb[:, b, hsl], in1=psA[:, fbk, :], op=ALU.add),
                        reads=[("x", b), ("psA", fb[b])], writes=[("x", b)])

            for b in range(NB):
                norm_elem(b)

            def stage_G(b):
                cols = slice(b * 128, (b + 1) * 128)
                for half in range(2):
                    hsl = slice(half * 512, (half + 1) * 512)
                    bg = bankA()
                    for kc in range(8):
                        S.add("pe", lambda e, kc=kc, bg=bg, hsl=hsl: e.matmul(
                            psA[:, bg, :], actT[:, kc, cols], wgs[:, kc, hsl],
                            start=(kc == 0), stop=(kc == 7)),
                            reads=[("actT", b), "wgs"], writes=[("psA", bg)])
                    bp = bankA()
                    for kc in range(2):
                        S.add("pe", lambda e, kc=kc, bp=bp, hsl=hsl: e.matmul(
                            psA[:, bp, :], pT[:, kc, cols], wple[:, kc, hsl],
                            start=(kc == 0), stop=(kc == 1)),
                            reads=[("pT", b), "wple"], writes=[("psA", bp)])
                    sl = scr_slot()
                    gt = scr[:, sl, 0:512]
                    S.add("act", lambda e, bg=bg, gt=gt: e.activation(out=gt, in_=psA[:, bg, :], func=AF.Sigmoid),
                          reads=[("psA", bg)], writes=[("scr", sl)])
                    S.add("dve", lambda e, bp=bp, gt=gt: e.tensor_tensor(out=gt, in0=gt, in1=psA[:, bp, :],
                                                                         op=ALU.mult),
                          reads=[("scr", sl), ("psA", bp)], writes=[("scr", sl)])
                    S.add("pool", lambda e, gt=gt, hsl=hsl: e.tensor_tensor(out=xb[:, b, hsl], in0=xb[:, b, hsl],
                                                                            in1=gt, op=ALU.add),
                          reads=[("x", b), ("scr", sl)], writes=[("x", b)])

            def stage_H(b):
                ss = stats[:, 8 + b:9 + b]
                rs = stats[:, 12 + b:13 + b]
                sl = scr_slot()
                junk = scr[:, sl, :].bitcast(BF16)[:, 0:1024]
                S.add("act", lambda e: e.activation(out=junk, in_=xb[:, b, :], func=AF.Square, accum_out=ss),
                      reads=[("x", b)], writes=[("scr", sl), ("ssH", b)])
                rstd_from_ss(ss, rs, stats[:, 40 + b:41 + b], stats[:, 44 + b:45 + b], 1, float(D),
                             [("ssH", b)], [("rsH", b)])
                sl2 = scr_slot()
                S.add("dve", lambda e: e.scalar_tensor_tensor(out=scr[:, sl2, :], in0=xb[:, b, :], scalar=rs,
                                                              in1=gfin[:, :], op0=ALU.mult, op1=ALU.mult),
                      reads=[("x", b), ("rsH", b), "gfin"], writes=[("scr", sl2)])
                r0 = t * TT + b * 128
                S.add("sp", lambda e: e.dma_start(out=out_d[r0:r0 + 128, :], in_=scr[:, sl2, :]),
                      reads=[("scr", sl2)], dma=("st", sl2))
                if t + 1 < NT:
                    load_x(t + 1, b)

            gorder = [("T", 0), ("T", 1), ("G", 0), ("T", 2), ("G", 1), ("H", 0), ("T", 3), ("G", 2), ("H", 1),
                      ("A", 0), ("G", 3), ("H", 2), ("A", 1), ("H", 3), ("A", 2), ("A", 3)]
            for nm, b in gorder:
                if nm == "T":
                    norm_transpose(b, 16)
                elif nm == "G":
                    stage_G(b)
                elif nm == "H":
                    stage_H(b)
                elif t + 1 < NT:
                    norm_elem(b)

        S.finalize()
        store_keys = [k for k in S.dma_cum if isinstance(k, tuple) and k[0] == "st"]

        sems = {}
        for e in ("pe", "act", "dve", "pool"):
            sems[e] = es.enter_context(nc.semaphore("s_" + e))
        for i, k in enumerate(S.dma_cum):
            sems[("dma", k)] = es.enter_context(nc.semaphore("d%d" % i))
        for k, v in S.dma_cum.items():
            assert v < 60000, (k, v)

        with nc.Block() as block:
            @block.tensor
            def _(eobj):
                _emit_engine("pe", eobj, S, sems)

            @block.scalar
            def _(eobj):
                _emit_engine("act", eobj, S, sems)

            @block.vector
            def _(eobj):
                _emit_engine("dve", eobj, S, sems)

            @block.gpsimd
            def _(eobj):
                _emit_engine("pool", eobj, S, sems)

            @block.sync
            def _(eobj):
                _emit_engine("sp", eobj, S, sems)
                for k in store_keys:
                    eobj.wait_ge(sems[("dma", k)], S.dma_cum[k])
    return nc


def make_in_maps(inputs, n_cores, nseq):
    f32 = np.float32
    c = _host_consts()
    x = np.asarray(inputs["x"], f32)
    p = np.asarray(inputs["p"], f32)[0]
    g3 = np.stack([np.asarray(inputs[k], f32)[0].reshape(8, 128).T for k in ("g_mix", "g_ffn", "g_ple")], axis=1)
    gT = np.ascontiguousarray(g3.reshape(128, 24))
    shared = {
        "w_in": np.ascontiguousarray(np.asarray(inputs["w_in"], f32)[0]),
        "w_out": np.ascontiguousarray(np.asarray(inputs["w_out"], f32)[0]),
        "w_ff1": np.ascontiguousarray(np.asarray(inputs["w_ff1"], f32)[0]),
        "w_ff2": np.ascontiguousarray(np.asarray(inputs["w_ff2"], f32)[0]),
        "w_gate": np.ascontiguousarray(np.asarray(inputs["w_ple_gate"], f32)[0]),
        "w_ple": np.ascontiguousarray(np.asarray(inputs["w_ple"], f32)[0]),
        "gT": gT,
        "gfin_b": np.ascontiguousarray(np.broadcast_to(np.asarray(inputs["g_final"], f32)[None, :], (128, D))),
        "retg_b": np.ascontiguousarray(np.broadcast_to(np.asarray(inputs["ret_norm_g"], f32)[0][None, :], (128, 512))),
        "sgg_b": np.ascontiguousarray(np.broadcast_to(np.asarray(inputs["sg_norm_g"], f32)[0][None, :], (128, 512))),
        "bsT": np.ascontiguousarray(np.asarray(inputs["b_s"], f32)[0].T),
        "wsT": np.ascontiguousarray(np.asarray(inputs["w_s"], f32)[0].transpose(2, 0, 1).reshape(128, 512)),
        "DTm": c["DTm"], "KD": c["KD"], "EPSQ": c["EPSQ"], "MASKT": c["MASKT"],
        "ROT": c["ROT"], "IDENT": c["IDENT"],
    }
    in_maps = []
    for ci in range(n_cores):
        b0 = ci * nseq
        m = dict(shared)
        m["x"] = np.ascontiguousarray(x[b0:b0 + nseq].reshape(nseq * SEQ, D))
        m["p"] = np.ascontiguousarray(p[b0:b0 + nseq].reshape(nseq * SEQ, 256))
        in_maps.append(m)
    return in_maps


def kernel(x, p, g_mix, w_in, ret_norm_g, sg_norm_g, w_s, b_s, w_out, g_ffn,
           w_ff1, w_ff2, g_ple, w_ple_gate, w_ple, g_final):
    inputs = dict(x=x, p=p, g_mix=g_mix, w_in=w_in, ret_norm_g=ret_norm_g, sg_norm_g=sg_norm_g, w_s=w_s,
                  b_s=b_s, w_out=w_out, g_ffn=g_ffn, w_ff1=w_ff1, w_ff2=w_ff2, g_ple=g_ple,
                  w_ple_gate=w_ple_gate, w_ple=w_ple, g_final=g_final)
    nc = build(SEQ_PER_CORE)
    in_maps = make_in_maps(inputs, N_CORES, SEQ_PER_CORE)
    res = run_bass_kernel_spmd(nc, in_maps, core_ids=list(range(N_CORES)))
    outs = [np.asarray(r["out"], np.float32).reshape(SEQ_PER_CORE, SEQ, D) for r in res.results]
    return np.concatenate(outs, axis=0)
```

```python
import contextlib
import numpy as np
import concourse.bass as bass
import concourse.mybir as mybir
from concourse.bass_utils import run_bass_kernel_spmd

F32 = mybir.dt.float32
BF16 = mybir.dt.bfloat16
I32 = mybir.dt.int32
AF = mybir.ActivationFunctionType
ALU = mybir.AluOpType

D = 1024
SEQ = 2048
TT = 512
NB = 4
EPS = 1e-6
NRING = 3
N_CORES = 8
SEQ_PER_CORE = 4


class Op:
    __slots__ = ("eng", "fn", "dma", "ndma", "key", "val", "marked", "waits", "clock", "rank")


class Sched:
    ENGS = ("pe", "act", "dve", "pool", "sp")

    def __init__(self):
        self.eng_ops = {e: [] for e in self.ENGS}
        self.comp_ops = {e: [] for e in self.ENGS}
        self.issue_clock = {e: {} for e in self.ENGS}
        self.last_writer = {}
        self.readers = {}
        self.dma_cum = {}
        self.n_ops = 0

    def add(self, eng, fn, reads=(), writes=(), dma=None, ndma=1):
        op = Op()
        op.eng, op.fn, op.dma, op.ndma = eng, fn, dma, ndma
        op.marked = False
        self.eng_ops[eng].append(op)
        self.n_ops += 1
        if dma is None:
            self.comp_ops[eng].append(op)
            op.key, op.val = eng, len(self.comp_ops[eng])
        else:
            cum = self.dma_cum.get(dma, 0) + 16 * ndma
            self.dma_cum[dma] = cum
            op.key, op.val = ("dma", dma), cum
        deps = []
        for b in reads:
            w = self.last_writer.get(b)
            if w is not None:
                deps.append((w, 0))
        for b in writes:
            w = self.last_writer.get(b)
            if w is not None:
                deps.append((w, 1))
            for r in self.readers.get(b, {}).values():
                deps.append((r, 2))
        clk = self.issue_clock[eng]
        waits = {}
        for d, kind in deps:
            if d is op:
                continue
            if d.dma is None and d.eng == eng and dma is None:
                if eng == "pe" or kind != 0:
                    continue
            if clk.get(d.key, 0) >= d.val:
                continue
            if waits.get(d.key, 0) < d.val:
                waits[d.key] = d.val
            d.marked = True
            for k, v in d.clock.items():
                if clk.get(k, 0) < v:
                    clk[k] = v
        op.waits = list(waits.items())
        op.clock = dict(clk)
        op.clock[op.key] = op.val
        for b in reads:
            self.readers.setdefault(b, {})[op.key] = op
        for b in writes:
            self.last_writer[b] = op
            self.readers[b] = {}
        return op

    def finalize(self):
        for e in self.ENGS:
            r = 0
            for op in self.comp_ops[e]:
                if op.marked:
                    r += 1
                op.rank = r


def _emit_engine(eng_name, eobj, sched, sems):
    comp = sched.comp_ops
    for op in sched.eng_ops[eng_name]:
        for key, val in op.waits:
            if isinstance(key, tuple):
                eobj.wait_ge(sems[key], val)
            else:
                eobj.wait_ge(sems[key], comp[key][val - 1].rank)
        ins = op.fn(eobj)
        if op.dma is not None:
            if not isinstance(ins, (list, tuple)):
                ins = [ins]
            assert len(ins) == op.ndma
            for i_ in ins:
                i_.then_inc(sems[op.key], 16)
        elif op.marked:
            ins.then_inc(sems[op.key], 1)


def _host_consts():
    f32 = np.float32
    H = 4
    gam = (1.0 - 2.0 ** (-5.0 - np.arange(H, dtype=np.float64)))
    lg = np.log(gam)
    t = np.arange(128, dtype=np.float64)
    ci = (t // 64)[:, None]
    i_idx = t[None, :]
    j_idx = t[:, None]
    cj = (t // 64)[:, None]
    cii = (t // 64)[None, :]
    s = 128.0 ** -0.5
    DTm = np.zeros((128, H, 128), np.float64)
    for h in range(H):
        same = (cj == cii)
        earlier = (cj < cii)
        w = np.where(same, np.exp(np.abs(i_idx - j_idx) * lg[h]),
                     np.where(earlier, np.exp((i_idx - j_idx) * lg[h]), 0.0))
        qdec = np.exp((i_idx + 1.0) * lg[h])
        DTm[:, h, :] = s * w / qdec
    KD = np.stack([s * np.exp((127.0 - t) * lg[h]) for h in range(H)], axis=1)
    EPSQ = np.stack([EPS / np.exp(2.0 * (t + 1.0) * lg[h]) for h in range(H)], axis=1)
    MASKT = (cj <= cii).astype(np.float64)
    freqs = (10000.0 ** (-np.arange(0, 128, 2, dtype=f32) / f32(128))).astype(f32)
    pos = np.arange(SEQ, dtype=f32)
    ang = (pos[:, None] * freqs[None, :]).astype(f32)
    cos = np.cos(ang).astype(f32).T
    sin = np.sin(ang).astype(f32).T
    CC = np.concatenate([cos, cos], axis=0)
    SW = np.concatenate([sin, -sin], axis=0)
    ROT = np.zeros((SEQ // TT, 128, 2, TT), f32)
    for tp in range(SEQ // TT):
        ROT[tp, :, 0, :] = CC[:, tp * TT:(tp + 1) * TT]
        ROT[tp, :, 1, :] = SW[:, tp * TT:(tp + 1) * TT]
    return dict(
        DTm=DTm.reshape(128, 512).astype(f32), KD=KD.astype(f32), EPSQ=EPSQ.astype(f32),
        MASKT=MASKT.astype(f32), ROT=ROT,
        IDENT=np.eye(128, dtype=f32),
    )


def build(nseq):
    NT = nseq * (SEQ // TT)
    NTOK = NT * TT
    nc = bass.Bass("TRN2", target_bir_lowering=False)

    def din(name, shape, dt=F32):
        return nc.dram_tensor(name, list(shape), dt, kind="ExternalInput").ap()

    x_d = din("x", [NTOK, D])
    p_d = din("p", [NTOK, 256])
    w_in_d = din("w_in", [D, 3072])
    w_out_d = din("w_out", [D, D])
    w1_d = din("w_ff1", [D, 4096])
    w2_d = din("w_ff2", [4096, D])
    wg_d = din("w_gate", [D, D])
    wple_d = din("w_ple", [256, D])
    gT_d = din("gT", [128, 24])
    gfin_d = din("gfin_b", [128, D])
    retg_d = din("retg_b", [128, 512])
    sgg_d = din("sgg_b", [128, 512])
    bsT_d = din("bsT", [128, 4])
    wsT_d = din("wsT", [128, 512])
    DTm_d = din("DTm", [128, 512])
    KD_d = din("KD", [128, 4])
    EPSQ_d = din("EPSQ", [128, 4])
    MASKT_d = din("MASKT", [128, 128])
    ROT_d = din("ROT", [SEQ // TT, 128, 2, TT])
    IDENT_d = din("IDENT", [128, 128])
    out_d = nc.dram_tensor("out", [NTOK, D], F32, kind="ExternalOutput").ap()
    wqkb_d = nc.dram_tensor("wqkb", [2, D, 512], BF16, kind="Internal").ap()
    w1b_d = nc.dram_tensor("w1b", [8, D, 512], BF16, kind="Internal").ap()
    w2b_d = nc.dram_tensor("w2b", [8, D, 512], BF16, kind="Internal").ap()

    S = Sched()
    es = contextlib.ExitStack()

    def sb(name, shape, dt):
        return es.enter_context(nc.sbuf_tensor(name, list(shape), dt))

    with es:
        win = sb("win", [128, 8, 2048], BF16)
        wout = sb("wout", [128, 8, 1024], BF16)
        wgs = sb("wgs", [128, 8, 1024], BF16)
        wple = sb("wple", [128, 2, 1024], BF16)
        ring = sb("ring", [128, NRING, 8, 512], BF16)
        xb = sb("xb", [128, 6, D], F32)
        actT = sb("actT", [128, 8, TT], BF16)
        big = sb("big", [128, 16384], BF16)
        rot = sb("rot", [128, 2, TT], F32)
        pbf = sb("pbf", [128, NB, 256], BF16)
        pT = sb("pT", [128, 2, TT], BF16)
        scr = sb("scr", [128, 4, 1024], F32)
        xsb = sb("xsb", [128, NB, D], BF16)
        Sst = sb("Sst", [128, 512], F32)
        Sbf = sb("Sbf", [128, 2, 512], BF16)
        ident = sb("ident", [128, 128], BF16)
        gT = sb("gT_s", [128, 24], F32)
        gfin = sb("gfin_s", [128, D], F32)
        retg = sb("retg_s", [128, 512], F32)
        sgg = sb("sgg_s", [128, 512], F32)
        bsT = sb("bsT_s", [128, 4], F32)
        wsTf = sb("wsTf", [128, 512], F32)
        wsTb = sb("wsTb", [128, 512], BF16)
        DTm = sb("DTm_s", [128, 512], F32)
        KD = sb("KD_s", [128, 4], F32)
        EPSQ = sb("EPSQ_s", [128, 4], F32)
        MASKT = sb("MASKT_s", [128, 128], F32)
        stats = sb("stats", [128, 96], F32)
        st6 = sb("st6", [128, 2, 4, 6], F32)
        mv4 = sb("mv4", [128, 2, 4, 2], F32)
        psA = es.enter_context(nc.psum_tensor("psA", [128, 6, 512], F32))
        psT = es.enter_context(nc.psum_tensor("psT", [128, 2, 1024], BF16))

        hT = big[:, :].rearrange("p (f n) -> p f n", n=TT)
        qk = big[:, 0:4096].rearrange("p (c n) -> p c n", n=TT)
        OV = 4096

        def ov_bf(off, n):
            return big[:, OV + off: OV + off + n]

        def ov_f32(off, n):
            return big[:, OV + off: OV + off + 2 * n].bitcast(F32)

        PBSZ = 5120
        v_bf = [ov_bf(pb * PBSZ + 0, 512) for pb in range(2)]
        svn_bf = [ov_bf(pb * PBSZ + 512, 512) for pb in range(2)]
        ktok = [ov_bf(pb * PBSZ + 1024, 512) for pb in range(2)]
        PT = [ov_bf(pb * PBSZ + 1536, 512) for pb in range(2)]
        ybf = [ov_bf(pb * PBSZ + 2048, 1024) for pb in range(2)]
        gsb = [ov_f32(pb * PBSZ + 3072, 512) for pb in range(2)]
        ugb = [ov_f32(pb * PBSZ + 4096, 512) for pb in range(2)]
        assert OV + 2 * PBSZ <= 16384

        state = {"bank": 0, "tb": 0, "scr": 0, "unit": 0}

        def bankA():
            b = state["bank"]
            state["bank"] = (b + 1) % 6
            return b

        def bankT():
            b = state["tb"]
            state["tb"] = (b + 1) % 2
            return b

        def scr_slot():
            s_ = state["scr"]
            state["scr"] = (s_ + 1) % 4
            return s_

        def ld(eng, dst, src, key, wkey):
            S.add(eng, lambda e, dst=dst, src=src: e.dma_start(out=dst, in_=src), writes=[wkey], dma=key)

        ld("sp", gT[:, :], gT_d[:, :], "c_gT", "gT")
        ld("sp", DTm[:, :], DTm_d[:, :], "c_DTm", "DTm")
        ld("sp", KD[:, :], KD_d[:, :], "c_KD", "KD")
        ld("sp", EPSQ[:, :], EPSQ_d[:, :], "c_EPSQ", "EPSQ")
        ld("sp", MASKT[:, :], MASKT_d[:, :], "c_MASKT", "MASKT")
        ld("sp", wsTf[:, :], wsT_d[:, :], "c_wsT", "wsTf")
        ld("sp", bsT[:, :], bsT_d[:, :], "c_bsT", "bsT")
        ld("sp", retg[:, :], retg_d[:, :], "c_retg", "retg")
        ld("sp", sgg[:, :], sgg_d[:, :], "c_sgg", "sgg")
        ld("sp", gfin[:, :], gfin_d[:, :], "c_gfin", "gfin")
        ld("pool", ident[:, :], IDENT_d[:, :], "c_ident", "ident")
        S.add("dve", lambda e: e.tensor_tensor(
            out=wsTb[:, :].rearrange("p (g i) -> p g i", g=4),
            in0=wsTf[:, :].rearrange("p (g i) -> p g i", g=4),
            in1=MASKT[:, None, :].to_broadcast([128, 4, 128]), op=ALU.mult),
            reads=["wsTf", "MASKT"], writes=["wsTb"])

        units = []
        for t in range(NT):
            units += [("qk", 0), ("qk", 1)]
            units += [("w1", u) for u in range(8)]
            units += [("w2", u) for u in range(8)]
        unit_src = {"qk": wqkb_d, "w1": w1b_d, "w2": w2b_d}

        def issue_unit(n):
            if n >= len(units):
                return
            kind, u = units[n]
            slot = n % NRING
            src = unit_src[kind][u].rearrange("(kc p) n -> p kc n", p=128)
            S.add("sp", lambda e, slot=slot, src=src: e.dma_start(out=ring[:, slot, :, :], in_=src),
                  reads=[(kind + "b", u)], writes=[("ring", slot)], dma=("ring", slot))

        def next_unit():
            n = state["unit"]
            state["unit"] = n + 1
            issue_unit(n + NRING - 1)
            return n % NRING

        def cast_scratch(dst, src, key, wkey):
            S.add("pool", lambda e, dst=dst, src=src: e.dma_start(out=dst, in_=src), writes=[wkey], dma=key)

        for u in range(2):
            cast_scratch(wqkb_d[u], w_in_d[:, u * 512:(u + 1) * 512], "cs_qk%d" % u, ("qkb", u))
        S.add("pool", lambda e: e.dma_start(
            out=win[:, :, :], in_=w_in_d[:, 1024:3072].rearrange("(kc p) n -> p kc n", p=128)),
            writes=["win"], dma="cs_win")
        S.add("pool", lambda e: e.dma_start(
            out=wout[:, :, :], in_=w_out_d.rearrange("(kc p) n -> p kc n", p=128)),
            writes=["wout"], dma="cs_wout")
        for u in range(8):
            cast_scratch(w1b_d[u], w1_d[:, u * 512:(u + 1) * 512], "cs_w1_%d" % u, ("w1b", u))
        for u in range(8):
            dh, fg = u // 4, u % 4
            cast_scratch(w2b_d[u], w2_d[fg * 1024:(fg + 1) * 1024, dh * 512:(dh + 1) * 512],
                         "cs_w2_%d" % u, ("w2b", u))
        S.add("pool", lambda e: e.dma_start(
            out=wgs[:, :, :], in_=wg_d.rearrange("(kc p) n -> p kc n", p=128)),
            writes=["wgs"], dma="cs_wg")
        S.add("pool", lambda e: e.dma_start(
            out=wple[:, :, :], in_=wple_d.rearrange("(kc p) n -> p kc n", p=128)),
            writes=["wple"], dma="cs_wple")

        def xslot(t, b):
            return (4 * t + b) % 6

        def load_x(t, b):
            r0 = t * TT + b * 128
            xs_ = xslot(t, b)
            S.add("sp", lambda e, r0=r0, xs_=xs_: e.dma_start(out=xb[:, xs_, :], in_=x_d[r0:r0 + 128, :]),
                  writes=[("x", xs_)], dma=("xl", xs_))

        def load_p(t):
            r0 = t * TT
            S.add("pool", lambda e, r0=r0: e.dma_start(
                out=pbf[:, :, :], in_=p_d[r0:r0 + TT, :].rearrange("(b p) c -> p b c", p=128)),
                writes=["pbf"], dma="pl")

        def load_rot(t):
            tp = t % (SEQ // TT)
            S.add("sp", lambda e, tp=tp: e.dma_start(out=rot[:, :, :], in_=ROT_d[tp]),
                  writes=["rot"], dma="rl")

        for b in range(NB):
            load_x(0, b)
        load_rot(0)
        load_p(0)
        for n in range(NRING - 1):
            issue_unit(n)
        PRO = {"pending": True}

        MAGIC = 0x5F3759DF
        CDEC_H = [float(np.exp(128.0 * np.log(1.0 - 2.0 ** (-5.0 - h)))) for h in range(4)]

        def rsqrt_dve(a_ap, y_ap, t_ap, k, keys):
            S.add("act", lambda e: e.activation(out=y_ap, in_=a_ap, func=AF.Sqrt), reads=keys, writes=keys)
            S.add("dve", lambda e: e.reciprocal(out=y_ap, in_=y_ap), reads=keys, writes=keys)

        def rstd_from_ss(ss_ap, rs_ap, a_ap, t_ap, k, n, rkeys, wkeys):
            S.add("dve", lambda e: e.tensor_scalar(out=a_ap, in0=ss_ap, scalar1=1.0 / n, scalar2=EPS,
                                                   op0=ALU.mult, op1=ALU.add), reads=rkeys, writes=wkeys)
            rsqrt_dve(a_ap, rs_ap, t_ap, k, wkeys)

        def norm_elem(b, xs_):
            S.add("act", lambda e: e.activation(out=xsb[:, b, :], in_=xb[:, xs_, :], func=AF.Square,
                                                accum_out=stats[:, b:b + 1]),
                  reads=[("x", xs_)], writes=[("xs", b), ("ss", b)])
            rstd_from_ss(stats[:, b:b + 1], stats[:, 4 + b:5 + b], stats[:, 32 + b:33 + b],
                         stats[:, 36 + b:37 + b], 1, float(D), [("ss", b)], [("rs", b)])
            S.add("act", lambda e: e.activation(out=xsb[:, b, :], in_=xb[:, xs_, :], func=AF.Identity,
                                                scale=stats[:, 4 + b:5 + b]),
                  reads=[("x", xs_), ("rs", b)], writes=[("xs", b)])

        def norm_transpose(b, gcol):
            tb = bankT()
            for kc in range(8):
                S.add("pe", lambda e, kc=kc: e.transpose(out=psT[:, tb, kc * 128:(kc + 1) * 128],
                                                         in_=xsb[:, b, kc * 128:(kc + 1) * 128],
                                                         identity=ident[:, :]),
                      reads=[("xs", b), "ident"], writes=[("psT", tb)])
            S.add("dve", lambda e: e.tensor_tensor(
                out=actT[:, :, b * 128:(b + 1) * 128],
                in0=psT[:, tb, :].rearrange("p (k n) -> p k n", k=8),
                in1=gT[:, gcol:gcol + 8, None].to_broadcast([128, 8, 128]), op=ALU.mult),
                reads=[("psT", tb), "gT"], writes=[("actT", b)])

        ALL_ACT = [("actT", b) for b in range(NB)]

        for t in range(NT):
            tp = t % (SEQ // TT)

            XS = [xslot(t, b) for b in range(NB)]
            if PRO["pending"]:
                for b in range(NB):
                    norm_elem(b, XS[b])
                PRO["pending"] = False
            if t + 1 < NT:
                load_x(t + 1, 0)
                load_x(t + 1, 1)
            def phase_B():
                for qk_i in range(2):
                    slot = next_unit()
                    for c4 in range(4):
                        c = qk_i * 4 + c4
                        bk = bankA()
                        for kc in range(8):
                            S.add("pe", lambda e, kc=kc, c4=c4, slot=slot, bk=bk: e.matmul(
                                psA[:, bk, :], ring[:, slot, kc, c4 * 128:(c4 + 1) * 128], actT[:, kc, :],
                                start=(kc == 0), stop=(kc == 7)),
                                reads=[("ring", slot)] + ALL_ACT, writes=[("psA", bk)])
                        sl = scr_slot()
                        m1 = scr[:, sl, 0:512]
                        m2 = scr[:, sl, 512:1024]
                        S.add("dve", lambda e, bk=bk, m1=m1: e.tensor_tensor(
                            out=m1, in0=psA[:, bk, :], in1=rot[:, 0, :], op=ALU.mult),
                            reads=[("psA", bk), "rot"], writes=[("scr", sl)])
                        S.add("dve", lambda e, bk=bk, m2=m2: e.tensor_tensor(
                            out=m2[0:64, :], in0=psA[64:128, bk, :], in1=rot[64:128, 1, :], op=ALU.mult),
                            reads=[("psA", bk), "rot"], writes=[("scr", sl)])
                        S.add("dve", lambda e, bk=bk, m2=m2: e.tensor_tensor(
                            out=m2[64:128, :], in0=psA[0:64, bk, :], in1=rot[0:64, 1, :], op=ALU.mult),
                            reads=[("psA", bk), "rot"], writes=[("scr", sl)])
                        S.add("pool", lambda e, c=c, m1=m1, m2=m2: e.tensor_tensor(
                            out=qk[:, c, :], in0=m1, in1=m2, op=ALU.add),
                            reads=[("scr", sl)], writes=[("qk", c)])
                if t + 1 < NT:
                    load_rot(t + 1)

            def stage_X1(b):
                pb = b % 2
                cols = slice(b * 128, (b + 1) * 128)
                banks = []
                for gi in range(4):
                    bk = bankA()
                    banks.append(bk)
                    for kc in range(8):
                        S.add("pe", lambda e, kc=kc, gi=gi, bk=bk: e.matmul(
                            psA[:, bk, :], actT[:, kc, cols], win[:, kc, gi * 512:(gi + 1) * 512],
                            start=(kc == 0), stop=(kc == 7)),
                            reads=[("actT", b), "win"], writes=[("psA", bk)])
                sl = scr_slot()
                svg = scr[:, sl, 0:512]
                S.add("act", lambda e: e.activation(out=v_bf[pb], in_=psA[:, banks[0], :], func=AF.Copy),
                      reads=[("psA", banks[0])], writes=[("v", pb)])

                def do_silu():
                    S.add("act", lambda e: e.activation(out=gsb[pb], in_=psA[:, banks[1], :], func=AF.Silu),
                          reads=[("psA", banks[1])], writes=[("gs", pb)])

                def do_gelu():
                    S.add("act", lambda e: e.activation(out=svg, in_=psA[:, banks[3], :],
                                                        func=AF.Gelu_apprx_tanh),
                          reads=[("psA", banks[3])], writes=[("scr", sl)])
                    S.add("act", lambda e: e.activation(out=ugb[pb], in_=psA[:, banks[2], :],
                                                        func=AF.Gelu_apprx_tanh),
                          reads=[("psA", banks[2])], writes=[("ug", pb)])
                if b % 2 == 0:
                    do_silu(); do_gelu()
                else:
                    do_gelu(); do_silu()
                S.add("pool", lambda e: e.tensor_tensor(out=gsb[pb], in0=gsb[pb], in1=retg[:, :], op=ALU.mult),
                      reads=[("gs", pb), "retg"], writes=[("gs", pb)])
                s6 = st6[:, pb, 0, :]
                mv = mv4[:, pb, 0, :]
                S.add("dve", lambda e: e.bn_stats(out=s6, in_=svg),
                      reads=[("scr", sl)], writes=[("st6", pb)])
                S.add("dve", lambda e: e.bn_aggr(out=mv, in_=s6), reads=[("st6", pb)], writes=[("mv", pb)])
                rs1 = stats[:, 16 + pb:17 + pb]
                a1 = stats[:, 48 + pb:49 + pb]
                S.add("dve", lambda e: e.tensor_scalar(out=a1, in0=mv4[:, pb, 0, 1:2], scalar1=EPS, scalar2=None,
                                                       op0=ALU.add),
                      reads=[("mv", pb)], writes=[("rs1", pb)])
                rsqrt_dve(a1, rs1, stats[:, 50 + pb:51 + pb], 1, [("rs1", pb)])
                S.add("dve", lambda e: e.tensor_scalar(out=svg, in0=svg, scalar1=mv4[:, pb, 0, 0:1], scalar2=rs1,
                                                       op0=ALU.subtract, op1=ALU.mult),
                      reads=[("scr", sl), ("mv", pb), ("rs1", pb)], writes=[("scr", sl)])
                S.add("pool", lambda e: e.tensor_tensor(out=svn_bf[pb], in0=svg, in1=sgg[:, :], op=ALU.mult),
                      reads=[("scr", sl), "sgg"], writes=[("svn", pb)])

            def stage_X2(b):
                pb = b % 2
                cols = slice(b * 128, (b + 1) * 128)
                last = (tp == SEQ // TT - 1 and b == NB - 1)
                if not last:
                    tb = bankT()
                    for h in range(4):
                        S.add("pe", lambda e, h=h: e.transpose(out=psT[:, tb, h * 128:(h + 1) * 128],
                                                               in_=qk[:, 4 + h, cols], identity=ident[:, :]),
                              reads=[("qk", 4 + h), "ident"], writes=[("psT", tb)])
                    S.add("dve", lambda e: e.tensor_tensor(
                        out=ktok[pb].rearrange("p (h d) -> p h d", h=4),
                        in0=psT[:, tb, 0:512].rearrange("p (h d) -> p h d", h=4),
                        in1=KD[:, :, None].to_broadcast([128, 4, 128]), op=ALU.mult),
                        reads=[("psT", tb), "KD"], writes=[("ktok", pb)])
                bk = bankA()
                for h in range(4):
                    S.add("pe", lambda e, h=h: e.matmul(psA[:, bk, h * 128:(h + 1) * 128],
                                                        qk[:, 4 + h, cols], qk[:, h, cols],
                                                        start=True, stop=True),
                          reads=[("qk", 4 + h), ("qk", h)], writes=[("psA", bk)])
                S.add("dve", lambda e: e.tensor_tensor(out=PT[pb], in0=psA[:, bk, :], in1=DTm[:, :], op=ALU.mult),
                      reads=[("psA", bk), "DTm"], writes=[("PT", pb)])

            def stage_X(b):
                stage_X1(b)
                stage_X2(b)

            def stage_Y(b):
                pb = b % 2
                cols = slice(b * 128, (b + 1) * 128)
                first = (tp == 0 and b == 0)
                last = (tp == SEQ // TT - 1 and b == NB - 1)
                gb = (t * NB + b)
                ps_in = gb % 2
                bo = bankA()
                for h in range(4):
                    hs = slice(h * 128, (h + 1) * 128)
                    S.add("pe", lambda e, hs=hs: e.matmul(psA[:, bo, hs], PT[pb][:, hs], v_bf[pb][:, hs],
                                                          start=True, stop=first),
                          reads=[("PT", pb), ("v", pb)], writes=[("psA", bo)])
                    if not first:
                        S.add("pe", lambda e, hs=hs, h=h: e.matmul(psA[:, bo, hs], qk[:, h, cols],
                                                                    Sbf[:, ps_in, hs], start=False, stop=True),
                              reads=[("qk", h), ("Sbf", ps_in)], writes=[("psA", bo)])
                if not last:
                    bu = bankA()
                    for h in range(4):
                        hs = slice(h * 128, (h + 1) * 128)
                        S.add("pe", lambda e, hs=hs: e.matmul(psA[:, bu, hs], ktok[pb][:, hs], v_bf[pb][:, hs],
                                                              start=True, stop=True),
                              reads=[("ktok", pb), ("v", pb)], writes=[("psA", bu)])
                    if first:
                        S.add("dve", lambda e: e.tensor_copy(out=Sst[:, :], in_=psA[:, bu, :]),
                              reads=[("psA", bu)], writes=["Sst"])
                    else:
                        for h in range(4):
                            hs = slice(h * 128, (h + 1) * 128)
                            S.add("dve", lambda e, hs=hs, h=h: e.scalar_tensor_tensor(
                                out=Sst[:, hs], in0=Sst[:, hs], scalar=float(CDEC_H[h]), in1=psA[:, bu, hs],
                                op0=ALU.mult, op1=ALU.add),
                                reads=["Sst", ("psA", bu)], writes=["Sst"])
                    S.add("act", lambda e: e.activation(out=Sbf[:, 1 - ps_in, :], in_=Sst[:, :], func=AF.Copy),
                          reads=["Sst"], writes=[("Sbf", 1 - ps_in)])
                for h in range(4):
                    S.add("dve", lambda e, h=h: e.bn_stats(out=st6[:, pb, h, :], in_=psA[:, bo, h * 128:(h + 1) * 128]),
                          reads=[("psA", bo)], writes=[("st6", pb)])
                for h in range(4):
                    S.add("dve", lambda e, h=h: e.bn_aggr(out=mv4[:, pb, h, :], in_=st6[:, pb, h, :]),
                          reads=[("st6", pb)], writes=[("mv", pb)])
                rs4 = stats[:, 20 + 4 * pb:24 + 4 * pb]
                a4 = stats[:, 64 + 4 * pb:68 + 4 * pb]
                S.add("dve", lambda e: e.tensor_tensor(out=a4, in0=mv4[:, pb, :, 1], in1=EPSQ[:, :], op=ALU.add),
                      reads=[("mv", pb), "EPSQ"], writes=[("rs4", pb)])
                rsqrt_dve(a4, rs4, stats[:, 72 + 4 * pb:76 + 4 * pb], 4, [("rs4", pb)])
                sl = scr_slot()
                rn = scr[:, sl, 0:512]
                for h in range(4):
                    hs = slice(h * 128, (h + 1) * 128)
                    S.add("dve", lambda e, hs=hs, h=h: e.tensor_scalar(
                        out=rn[:, hs], in0=psA[:, bo, hs], scalar1=mv4[:, pb, h, 0:1],
                        scalar2=stats[:, 20 + 4 * pb + h:21 + 4 * pb + h],
                        op0=ALU.subtract, op1=ALU.mult),
                        reads=[("psA", bo), ("mv", pb), ("rs4", pb)], writes=[("scr", sl)])
                S.add("pool", lambda e: e.tensor_tensor(out=ybf[pb][:, 0:512], in0=rn, in1=gsb[pb], op=ALU.mult),
                      reads=[("scr", sl), ("gs", pb)], writes=[("y", pb)])
                bs_ = bankA()
                for g in range(4):
                    hs = slice(g * 128, (g + 1) * 128)
                    S.add("pe", lambda e, hs=hs: e.matmul(psA[:, bs_, hs], wsTb[:, hs], svn_bf[pb][:, hs],
                                                          start=True, stop=True),
                          reads=["wsTb", ("svn", pb)], writes=[("psA", bs_)])
                tg = scr[:, sl, 512:1024]
                S.add("dve", lambda e: e.tensor_tensor(
                    out=tg.rearrange("p (g c) -> p g c", g=4),
                    in0=psA[:, bs_, :].rearrange("p (g c) -> p g c", g=4),
                    in1=bsT[:, :, None].to_broadcast([128, 4, 128]), op=ALU.add),
                    reads=[("psA", bs_), "bsT"], writes=[("scr", sl)])
                S.add("pool", lambda e: e.tensor_tensor(out=ybf[pb][:, 512:1024], in0=tg, in1=ugb[pb], op=ALU.mult),
                      reads=[("scr", sl), ("ug", pb)], writes=[("y", pb)])

            def stage_Z(b):
                pb = b % 2
                tb = bankT()
                for kc in range(8):
                    S.add("pe", lambda e, kc=kc: e.transpose(out=psT[:, tb, kc * 128:(kc + 1) * 128],
                                                             in_=ybf[pb][:, kc * 128:(kc + 1) * 128],
                                                             identity=ident[:, :]),
                          reads=[("y", pb), "ident"], writes=[("psT", tb)])
                S.add("act", lambda e: e.activation(out=actT[:, :, b * 128:(b + 1) * 128],
                                                    in_=psT[:, tb, :].rearrange("p (k n) -> p k n", k=8),
                                                    func=AF.Copy),
                      reads=[("psT", tb)], writes=[("actT", b)])

            def stage_W(b):
                cols = slice(b * 128, (b + 1) * 128)
                for half in range(2):
                    bk = bankA()
                    hsl = slice(half * 512, (half + 1) * 512)
                    for kc in range(8):
                        S.add("pe", lambda e, kc=kc, bk=bk, hsl=hsl: e.matmul(
                            psA[:, bk, :], actT[:, kc, cols], wout[:, kc, hsl],
                            start=(kc == 0), stop=(kc == 7)),
                            reads=[("actT", b), "wout"], writes=[("psA", bk)])
                    S.add("dve", lambda e, bk=bk, hsl=hsl, xs_=XS[b]: e.tensor_tensor(
                        out=xb[:, xs_, hsl], in0=xb[:, xs_, hsl], in1=psA[:, bk, :], op=ALU.add),
                        reads=[("x", XS[b]), ("psA", bk)], writes=[("x", XS[b])])
                norm_elem(b, XS[b])

            def stage_DT(b):
                norm_transpose(b, 8)

            norm_transpose(0, 0)
            norm_transpose(1, 0)
            stage_X1(0)
            stage_X1(1)
            norm_transpose(2, 0)
            norm_transpose(3, 0)
            phase_B()
            order = [("X2", 0), ("X2", 1), ("Y", 0), ("X", 2), ("Y", 1), ("Z", 0), ("X", 3), ("Y", 2), ("Z", 1),
                     ("W", 0), ("Y", 3), ("Z", 2), ("W", 1), ("DT", 0), ("Z", 3), ("W", 2), ("DT", 1), ("W", 3),
                     ("DT", 2), ("DT", 3)]
            fns = {"X": stage_X, "X2": stage_X2, "Y": stage_Y, "Z": stage_Z, "W": stage_W, "DT": stage_DT}
            for nm, b in order:
                fns[nm](b)

            for u in range(8):
                slot = next_unit()
                for f4 in range(4):
                    f = u * 4 + f4
                    bk = bankA()
                    for kc in range(8):
                        S.add("pe", lambda e, kc=kc, f4=f4, slot=slot, bk=bk: e.matmul(
                            psA[:, bk, :], ring[:, slot, kc, f4 * 128:(f4 + 1) * 128], actT[:, kc, :],
                            start=(kc == 0), stop=(kc == 7)),
                            reads=[("ring", slot)] + ALL_ACT, writes=[("psA", bk)])
                    sl = scr_slot()
                    r_ = scr[:, sl, 0:512]
                    S.add("act", lambda e, bk=bk, r_=r_: e.activation(out=r_, in_=psA[:, bk, :], func=AF.Relu),
                          reads=[("psA", bk)], writes=[("scr", sl)])
                    S.add("dve" if f % 2 == 0 else "pool",
                          lambda e, f=f, r_=r_: e.tensor_tensor(out=hT[:, f, :], in0=r_, in1=r_, op=ALU.mult),
                          reads=[("scr", sl)], writes=[("hT", f)])
                if u == 2:
                    for b in range(NB):
                        tb = bankT()
                        for kc in range(2):
                            S.add("pe", lambda e, kc=kc, b=b, tb=tb: e.transpose(
                                out=psT[:, tb, kc * 128:(kc + 1) * 128],
                                in_=pbf[:, b, kc * 128:(kc + 1) * 128], identity=ident[:, :]),
                                reads=["pbf", "ident"], writes=[("psT", tb)])
                        S.add("act", lambda e, b=b, tb=tb: e.activation(
                            out=pT[:, :, b * 128:(b + 1) * 128],
                            in_=psT[:, tb, 0:256].rearrange("p (k n) -> p k n", k=2), func=AF.Copy),
                            reads=[("psT", tb)], writes=[("pT", b)])
                    if t + 1 < NT:
                        load_p(t + 1)

            for dh in range(2):
                hsl = slice(dh * 512, (dh + 1) * 512)
                fb = [bankA() for _ in range(NB)]
                for fg in range(4):
                    slot = next_unit()
                    for b in range(NB):
                        cols = slice(b * 128, (b + 1) * 128)
                        for fc in range(8):
                            f = fg * 8 + fc
                            S.add("pe", lambda e, f=f, fc=fc, slot=slot, fbk=fb[b], cols=cols: e.matmul(
                                psA[:, fbk, :], hT[:, f, cols], ring[:, slot, fc, :],
                                start=(f == 0), stop=(f == 31)),
                                reads=[("hT", f), ("ring", slot)], writes=[("psA", fb[b])])
                for b in range(NB):
                    S.add("dve", lambda e, xs_=XS[b], hsl=hsl, fbk=fb[b]: e.tensor_tensor(
                        out=xb[:, xs_, hsl], in0=xb[:, xs_, hsl], in1=psA[:, fbk, :], op=ALU.add),
                        reads=[("x", XS[b]), ("psA", fb[b])], writes=[("x", XS[b])])

            for b in range(NB):
                norm_elem(b, XS[b])

            def stage_G(b):
                cols = slice(b * 128, (b + 1) * 128)
                for half in range(2):
                    hsl = slice(half * 512, (half + 1) * 512)
                    bg = bankA()
                    for kc in range(8):
                        S.add("pe", lambda e, kc=kc, bg=bg, hsl=hsl: e.matmul(
                            psA[:, bg, :], actT[:, kc, cols], wgs[:, kc, hsl],
                            start=(kc == 0), stop=(kc == 7)),
                            reads=[("actT", b), "wgs"], writes=[("psA", bg)])
                    bp = bankA()
                    for kc in range(2):
                        S.add("pe", lambda e, kc=kc, bp=bp, hsl=hsl: e.matmul(
                            psA[:, bp, :], pT[:, kc, cols], wple[:, kc, hsl],
                            start=(kc == 0), stop=(kc == 1)),
                            reads=[("pT", b), "wple"], writes=[("psA", bp)])
                    sl = scr_slot()
                    gt = scr[:, sl, 0:512]
                    S.add("act", lambda e, bg=bg, gt=gt: e.activation(out=gt, in_=psA[:, bg, :], func=AF.Sigmoid),
                          reads=[("psA", bg)], writes=[("scr", sl)])
                    S.add("dve", lambda e, bp=bp, gt=gt: e.tensor_tensor(out=gt, in0=gt, in1=psA[:, bp, :],
                                                                         op=ALU.mult),
                          reads=[("scr", sl), ("psA", bp)], writes=[("scr", sl)])
                    S.add("pool", lambda e, gt=gt, hsl=hsl, xs_=XS[b]: e.tensor_tensor(out=xb[:, xs_, hsl],
                                                                                        in0=xb[:, xs_, hsl],
                                                                                        in1=gt, op=ALU.add),
                          reads=[("x", XS[b]), ("scr", sl)], writes=[("x", XS[b])])

            def stage_H(b):
                ss = stats[:, 8 + b:9 + b]
                rs = stats[:, 12 + b:13 + b]
                sl = scr_slot()
                junk = scr[:, sl, :].bitcast(BF16)[:, 0:1024]
                S.add("act", lambda e, xs_=XS[b]: e.activation(out=junk, in_=xb[:, xs_, :], func=AF.Square,
                                                               accum_out=ss),
                      reads=[("x", XS[b])], writes=[("scr", sl), ("ssH", b)])
                rstd_from_ss(ss, rs, stats[:, 40 + b:41 + b], stats[:, 44 + b:45 + b], 1, float(D),
                             [("ssH", b)], [("rsH", b)])
                sl2 = scr_slot()
                S.add("dve", lambda e, xs_=XS[b]: e.scalar_tensor_tensor(out=scr[:, sl2, :], in0=xb[:, xs_, :], scalar=rs,
                                                              in1=gfin[:, :], op0=ALU.mult, op1=ALU.mult),
                      reads=[("x", XS[b]), ("rsH", b), "gfin"], writes=[("scr", sl2)])
                r0 = t * TT + b * 128
                S.add("sp", lambda e: e.dma_start(out=out_d[r0:r0 + 128, :], in_=scr[:, sl2, :]),
                      reads=[("scr", sl2)], dma=("st", sl2))
                if t + 1 < NT and b < 2:
                    load_x(t + 1, b + 2)

            gorder = [("T", 0), ("T", 1), ("G", 0), ("T", 2), ("G", 1), ("H", 0), ("T", 3), ("G", 2), ("A", 0),
                      ("H", 1), ("G", 3), ("A", 1), ("H", 2), ("A", 2), ("H", 3), ("A", 3)]
            for nm, b in gorder:
                if nm == "T":
                    norm_transpose(b, 16)
                elif nm == "G":
                    stage_G(b)
                elif nm == "H":
                    stage_H(b)
                elif t + 1 < NT:
                    norm_elem(b, xslot(t + 1, b))

        S.finalize()
        store_keys = [k for k in S.dma_cum if isinstance(k, tuple) and k[0] == "st"]

        sems = {}
        for e in ("pe", "act", "dve", "pool"):
            sems[e] = es.enter_context(nc.semaphore("s_" + e))
        for i, k in enumerate(S.dma_cum):
            sems[("dma", k)] = es.enter_context(nc.semaphore("d%d" % i))
        for k, v in S.dma_cum.items():
            assert v < 60000, (k, v)

        with nc.Block() as block:
            @block.tensor
            def _(eobj):
                _emit_engine("pe", eobj, S, sems)

            @block.scalar
            def _(eobj):
                _emit_engine("act", eobj, S, sems)

            @block.vector
            def _(eobj):
                _emit_engine("dve", eobj, S, sems)

            @block.gpsimd
            def _(eobj):
                _emit_engine("pool", eobj, S, sems)

            @block.sync
            def _(eobj):
                _emit_engine("sp", eobj, S, sems)
                for k in store_keys:
                    eobj.wait_ge(sems[("dma", k)], S.dma_cum[k])
    return nc


def make_in_maps(inputs, n_cores, nseq):
    f32 = np.float32
    c = _host_consts()
    x = np.asarray(inputs["x"], f32)
    p = np.asarray(inputs["p"], f32)[0]
    g3 = np.stack([np.asarray(inputs[k], f32)[0].reshape(8, 128).T for k in ("g_mix", "g_ffn", "g_ple")], axis=1)
    gT = np.ascontiguousarray(g3.reshape(128, 24))
    shared = {
        "w_in": np.ascontiguousarray(np.asarray(inputs["w_in"], f32)[0]),
        "w_out": np.ascontiguousarray(np.asarray(inputs["w_out"], f32)[0]),
        "w_ff1": np.ascontiguousarray(np.asarray(inputs["w_ff1"], f32)[0]),
        "w_ff2": np.ascontiguousarray(np.asarray(inputs["w_ff2"], f32)[0]),
        "w_gate": np.ascontiguousarray(np.asarray(inputs["w_ple_gate"], f32)[0]),
        "w_ple": np.ascontiguousarray(np.asarray(inputs["w_ple"], f32)[0]),
        "gT": gT,
        "gfin_b": np.ascontiguousarray(np.broadcast_to(np.asarray(inputs["g_final"], f32)[None, :], (128, D))),
        "retg_b": np.ascontiguousarray(np.broadcast_to(np.asarray(inputs["ret_norm_g"], f32)[0][None, :], (128, 512))),
        "sgg_b": np.ascontiguousarray(np.broadcast_to(np.asarray(inputs["sg_norm_g"], f32)[0][None, :], (128, 512))),
        "bsT": np.ascontiguousarray(np.asarray(inputs["b_s"], f32)[0].T),
        "wsT": np.ascontiguousarray(np.asarray(inputs["w_s"], f32)[0].transpose(2, 0, 1).reshape(128, 512)),
        "DTm": c["DTm"], "KD": c["KD"], "EPSQ": c["EPSQ"], "MASKT": c["MASKT"],
        "ROT": c["ROT"], "IDENT": c["IDENT"],
    }
    in_maps = []
    for ci in range(n_cores):
        b0 = ci * nseq
        m = dict(shared)
        m["x"] = np.ascontiguousarray(x[b0:b0 + nseq].reshape(nseq * SEQ, D))
        m["p"] = np.ascontiguousarray(p[b0:b0 + nseq].reshape(nseq * SEQ, 256))
        in_maps.append(m)
    return in_maps


def kernel(x, p, g_mix, w_in, ret_norm_g, sg_norm_g, w_s, b_s, w_out, g_ffn,
           w_ff1, w_ff2, g_ple, w_ple_gate, w_ple, g_final):
    inputs = dict(x=x, p=p, g_mix=g_mix, w_in=w_in, ret_norm_g=ret_norm_g, sg_norm_g=sg_norm_g, w_s=w_s,
                  b_s=b_s, w_out=w_out, g_ffn=g_ffn, w_ff1=w_ff1, w_ff2=w_ff2, g_ple=g_ple,
                  w_ple_gate=w_ple_gate, w_ple=w_ple, g_final=g_final)
    nc = build(SEQ_PER_CORE)
    in_maps = make_in_maps(inputs, N_CORES, SEQ_PER_CORE)
    res = run_bass_kernel_spmd(nc, in_maps, core_ids=list(range(N_CORES)))
    outs = [np.asarray(r["out"], np.float32).reshape(SEQ_PER_CORE, SEQ, D) for r in res.results]
    return np.concatenate(outs, axis=0)
```

```python
import contextlib
import numpy as np
import concourse.bass as bass
import concourse.mybir as mybir
from concourse.bass_utils import run_bass_kernel_spmd

F32 = mybir.dt.float32
BF16 = mybir.dt.bfloat16
I32 = mybir.dt.int32
AF = mybir.ActivationFunctionType
ALU = mybir.AluOpType

D = 1024
SEQ = 2048
TT = 512
NB = 4
EPS = 1e-6
NRING = 3
N_CORES = 8
SEQ_PER_CORE = 4


class Op:
    __slots__ = ("eng", "fn", "dma", "ndma", "key", "val", "marked", "waits", "clock", "rank")


class Sched:
    ENGS = ("pe", "act", "dve", "pool", "sp")

    def __init__(self):
        self.eng_ops = {e: [] for e in self.ENGS}
        self.comp_ops = {e: [] for e in self.ENGS}
        self.issue_clock = {e: {} for e in self.ENGS}
        self.last_writer = {}
        self.readers = {}
        self.dma_cum = {}
        self.n_ops = 0

    def add(self, eng, fn, reads=(), writes=(), dma=None, ndma=1):
        op = Op()
        op.eng, op.fn, op.dma, op.ndma = eng, fn, dma, ndma
        op.marked = False
        self.eng_ops[eng].append(op)
        self.n_ops += 1
        if dma is None:
            self.comp_ops[eng].append(op)
            op.key, op.val = eng, len(self.comp_ops[eng])
        else:
            cum = self.dma_cum.get(dma, 0) + 16 * ndma
            self.dma_cum[dma] = cum
            op.key, op.val = ("dma", dma), cum
        deps = []
        for b in reads:
            w = self.last_writer.get(b)
            if w is not None:
                deps.append((w, 0))
        for b in writes:
            w = self.last_writer.get(b)
            if w is not None:
                deps.append((w, 1))
            for r in self.readers.get(b, {}).values():
                deps.append((r, 2))
        clk = self.issue_clock[eng]
        waits = {}
        for d, kind in deps:
            if d is op:
                continue
            if d.dma is None and d.eng == eng and dma is None:
                if eng == "pe" or kind != 0:
                    continue
            if clk.get(d.key, 0) >= d.val:
                continue
            if waits.get(d.key, 0) < d.val:
                waits[d.key] = d.val
            d.marked = True
            for k, v in d.clock.items():
                if clk.get(k, 0) < v:
                    clk[k] = v
        op.waits = list(waits.items())
        op.clock = dict(clk)
        op.clock[op.key] = op.val
        for b in reads:
            self.readers.setdefault(b, {})[op.key] = op
        for b in writes:
            self.last_writer[b] = op
            self.readers[b] = {}
        return op

    def finalize(self):
        for e in self.ENGS:
            r = 0
            for op in self.comp_ops[e]:
                if op.marked:
                    r += 1
                op.rank = r


def _emit_engine(eng_name, eobj, sched, sems):
    comp = sched.comp_ops
    for op in sched.eng_ops[eng_name]:
        for key, val in op.waits:
            if isinstance(key, tuple):
                eobj.wait_ge(sems[key], val)
            else:
                eobj.wait_ge(sems[key], comp[key][val - 1].rank)
        ins = op.fn(eobj)
        if op.dma is not None:
            if not isinstance(ins, (list, tuple)):
                ins = [ins]
            assert len(ins) == op.ndma
            for i_ in ins:
                i_.then_inc(sems[op.key], 16)
        elif op.marked:
            ins.then_inc(sems[op.key], 1)


def _host_consts():
    f32 = np.float32
    H = 4
    gam = (1.0 - 2.0 ** (-5.0 - np.arange(H, dtype=np.float64)))
    lg = np.log(gam)
    t = np.arange(128, dtype=np.float64)
    ci = (t // 64)[:, None]
    i_idx = t[None, :]
    j_idx = t[:, None]
    cj = (t // 64)[:, None]
    cii = (t // 64)[None, :]
    s = 128.0 ** -0.5
    DTm = np.zeros((128, H, 128), np.float64)
    for h in range(H):
        same = (cj == cii)
        earlier = (cj < cii)
        w = np.where(same, np.exp(np.abs(i_idx - j_idx) * lg[h]),
                     np.where(earlier, np.exp((i_idx - j_idx) * lg[h]), 0.0))
        qdec = np.exp((i_idx + 1.0) * lg[h])
        DTm[:, h, :] = s * w / qdec
    KD = np.stack([s * np.exp((127.0 - t) * lg[h]) for h in range(H)], axis=1)
    EPSQ = np.stack([EPS / np.exp(2.0 * (t + 1.0) * lg[h]) for h in range(H)], axis=1)
    MASKT = (cj <= cii).astype(np.float64)
    freqs = (10000.0 ** (-np.arange(0, 128, 2, dtype=f32) / f32(128))).astype(f32)
    pos = np.arange(SEQ, dtype=f32)
    ang = (pos[:, None] * freqs[None, :]).astype(f32)
    cos = np.cos(ang).astype(f32).T
    sin = np.sin(ang).astype(f32).T
    CC = np.concatenate([cos, cos], axis=0)
    SW = np.concatenate([sin, -sin], axis=0)
    ROT = np.zeros((SEQ // TT, 128, 2, TT), f32)
    for tp in range(SEQ // TT):
        ROT[tp, :, 0, :] = CC[:, tp * TT:(tp + 1) * TT]
        ROT[tp, :, 1, :] = SW[:, tp * TT:(tp + 1) * TT]
    return dict(
        DTm=DTm.reshape(128, 512).astype(f32), KD=KD.astype(f32), EPSQ=EPSQ.astype(f32),
        MASKT=MASKT.astype(f32), ROT=ROT,
        IDENT=np.eye(128, dtype=f32),
    )


def build(nseq):
    NT = nseq * (SEQ // TT)
    NTOK = NT * TT
    nc = bass.Bass("TRN2", target_bir_lowering=False)

    def din(name, shape, dt=F32):
        return nc.dram_tensor(name, list(shape), dt, kind="ExternalInput").ap()

    x_d = din("x", [NTOK, D])
    p_d = din("p", [NTOK, 256])
    w_in_d = din("w_in", [D, 3072])
    w_out_d = din("w_out", [D, D])
    w1_d = din("w_ff1", [D, 4096])
    w2_d = din("w_ff2", [4096, D])
    wg_d = din("w_gate", [D, D])
    wple_d = din("w_ple", [256, D])
    gT_d = din("gT", [128, 24])
    gfin_d = din("gfin_b", [128, D])
    retg_d = din("retg_b", [128, 512])
    sgg_d = din("sgg_b", [128, 512])
    bsT_d = din("bsT", [128, 4])
    wsT_d = din("wsT", [128, 512])
    DTm_d = din("DTm", [128, 512])
    KD_d = din("KD", [128, 4])
    EPSQ_d = din("EPSQ", [128, 4])
    MASKT_d = din("MASKT", [128, 128])
    ROT_d = din("ROT", [SEQ // TT, 128, 2, TT])
    IDENT_d = din("IDENT", [128, 128])
    out_d = nc.dram_tensor("out", [NTOK, D], F32, kind="ExternalOutput").ap()
    wqkb_d = nc.dram_tensor("wqkb", [2, 128, 4096], BF16, kind="Internal").ap()
    w1b_d = nc.dram_tensor("w1b", [8, 128, 4096], BF16, kind="Internal").ap()
    w2b_d = nc.dram_tensor("w2b", [8, 128, 4096], BF16, kind="Internal").ap()

    S = Sched()
    es = contextlib.ExitStack()

    def sb(name, shape, dt):
        return es.enter_context(nc.sbuf_tensor(name, list(shape), dt))

    with es:
        win = sb("win", [128, 8, 2048], BF16)
        wout = sb("wout", [128, 8, 1024], BF16)
        wgs = sb("wgs", [128, 8, 1024], BF16)
        wple = sb("wple", [128, 2, 1024], BF16)
        ring = sb("ring", [128, NRING, 8, 512], BF16)
        xb = sb("xb", [128, 6, D], F32)
        actT = sb("actT", [128, 8, TT], BF16)
        big = sb("big", [128, 16384], BF16)
        rot = sb("rot", [128, 2, TT], F32)
        pbf = sb("pbf", [128, NB, 256], BF16)
        pT = sb("pT", [128, 2, TT], BF16)
        scr = sb("scr", [128, 4, 1024], F32)
        xsb = sb("xsb", [128, NB, D], BF16)
        Sst = sb("Sst", [128, 512], F32)
        Sbf = sb("Sbf", [128, 2, 512], BF16)
        ident = sb("ident", [128, 128], BF16)
        gT = sb("gT_s", [128, 24], F32)
        gfin = sb("gfin_s", [128, D], F32)
        retg = sb("retg_s", [128, 512], F32)
        sgg = sb("sgg_s", [128, 512], F32)
        bsT = sb("bsT_s", [128, 4], F32)
        wsTf = sb("wsTf", [128, 512], F32)
        wsTb = sb("wsTb", [128, 512], BF16)
        DTm = sb("DTm_s", [128, 512], F32)
        KD = sb("KD_s", [128, 4], F32)
        EPSQ = sb("EPSQ_s", [128, 4], F32)
        MASKT = sb("MASKT_s", [128, 128], F32)
        stats = sb("stats", [128, 96], F32)
        st6 = sb("st6", [128, 2, 4, 6], F32)
        mv4 = sb("mv4", [128, 2, 4, 2], F32)
        psA = es.enter_context(nc.psum_tensor("psA", [128, 6, 512], F32))
        psT = es.enter_context(nc.psum_tensor("psT", [128, 2, 1024], BF16))

        hT = big[:, :].rearrange("p (f n) -> p f n", n=TT)
        qk = big[:, 0:4096].rearrange("p (c n) -> p c n", n=TT)
        OV = 4096

        def ov_bf(off, n):
            return big[:, OV + off: OV + off + n]

        def ov_f32(off, n):
            return big[:, OV + off: OV + off + 2 * n].bitcast(F32)

        PBSZ = 5120
        v_bf = [ov_bf(pb * PBSZ + 0, 512) for pb in range(2)]
        svn_bf = [ov_bf(pb * PBSZ + 512, 512) for pb in range(2)]
        ktok = [ov_bf(pb * PBSZ + 1024, 512) for pb in range(2)]
        PT = [ov_bf(pb * PBSZ + 1536, 512) for pb in range(2)]
        ybf = [ov_bf(pb * PBSZ + 2048, 1024) for pb in range(2)]
        gsb = [ov_f32(pb * PBSZ + 3072, 512) for pb in range(2)]
        ugb = [ov_f32(pb * PBSZ + 4096, 512) for pb in range(2)]
        assert OV + 2 * PBSZ <= 16384

        state = {"bank": 0, "tb": 0, "scr": 0, "unit": 0}

        def bankA():
            b = state["bank"]
            state["bank"] = (b + 1) % 6
            return b

        def bankT():
            b = state["tb"]
            state["tb"] = (b + 1) % 2
            return b

        def scr_slot():
            s_ = state["scr"]
            state["scr"] = (s_ + 1) % 4
            return s_

        def ld(eng, dst, src, key, wkey):
            S.add(eng, lambda e, dst=dst, src=src: e.dma_start(out=dst, in_=src), writes=[wkey], dma=key)

        ld("sp", gT[:, :], gT_d[:, :], "c_gT", "gT")
        ld("sp", DTm[:, :], DTm_d[:, :], "c_DTm", "DTm")
        ld("sp", KD[:, :], KD_d[:, :], "c_KD", "KD")
        ld("sp", EPSQ[:, :], EPSQ_d[:, :], "c_EPSQ", "EPSQ")
        ld("sp", MASKT[:, :], MASKT_d[:, :], "c_MASKT", "MASKT")
        ld("sp", wsTf[:, :], wsT_d[:, :], "c_wsT", "wsTf")
        ld("sp", bsT[:, :], bsT_d[:, :], "c_bsT", "bsT")
        ld("sp", retg[:, :], retg_d[:, :], "c_retg", "retg")
        ld("sp", sgg[:, :], sgg_d[:, :], "c_sgg", "sgg")
        ld("sp", gfin[:, :], gfin_d[:, :], "c_gfin", "gfin")
        ld("pool", ident[:, :], IDENT_d[:, :], "c_ident", "ident")
        S.add("dve", lambda e: e.tensor_tensor(
            out=wsTb[:, :].rearrange("p (g i) -> p g i", g=4),
            in0=wsTf[:, :].rearrange("p (g i) -> p g i", g=4),
            in1=MASKT[:, None, :].to_broadcast([128, 4, 128]), op=ALU.mult),
            reads=["wsTf", "MASKT"], writes=["wsTb"])

        units = []
        for t in range(NT):
            units += [("qk", 0), ("qk", 1)]
            units += [("w1", u) for u in range(8)]
            units += [("w2", u) for u in range(8)]
        unit_src = {"qk": wqkb_d, "w1": w1b_d, "w2": w2b_d}

        def issue_unit(n):
            if n >= len(units):
                return
            kind, u = units[n]
            slot = n % NRING
            src = unit_src[kind][u].rearrange("p (kc n) -> p kc n", kc=8)
            S.add("sp", lambda e, slot=slot, src=src: e.dma_start(out=ring[:, slot, :, :], in_=src),
                  reads=[(kind + "b", u)], writes=[("ring", slot)], dma=("ring", slot))

        def next_unit():
            n = state["unit"]
            state["unit"] = n + 1
            issue_unit(n + NRING - 1)
            return n % NRING

        def cast_scratch(dst, src, key, wkey):
            S.add("pool", lambda e, dst=dst, src=src: e.dma_start(out=dst, in_=src), writes=[wkey], dma=key)

        def pm(ap):
            return ap.rearrange("p (kc n) -> p kc n", kc=8)

        def km(ap):
            return ap.rearrange("(kc p) n -> p kc n", p=128)

        for u in range(2):
            cast_scratch(pm(wqkb_d[u]), km(w_in_d[:, u * 512:(u + 1) * 512]), "cs_qk%d" % u, ("qkb", u))
        S.add("pool", lambda e: e.dma_start(
            out=win[:, :, :], in_=w_in_d[:, 1024:3072].rearrange("(kc p) n -> p kc n", p=128)),
            writes=["win"], dma="cs_win")
        S.add("pool", lambda e: e.dma_start(
            out=wout[:, :, :], in_=w_out_d.rearrange("(kc p) n -> p kc n", p=128)),
            writes=["wout"], dma="cs_wout")
        for u in range(8):
            cast_scratch(pm(w1b_d[u]), km(w1_d[:, u * 512:(u + 1) * 512]), "cs_w1_%d" % u, ("w1b", u))
        for u in range(8):
            dh, fg = u // 4, u % 4
            cast_scratch(pm(w2b_d[u]), km(w2_d[fg * 1024:(fg + 1) * 1024, dh * 512:(dh + 1) * 512]),
                         "cs_w2_%d" % u, ("w2b", u))
        S.add("pool", lambda e: e.dma_start(
            out=wgs[:, :, :], in_=wg_d.rearrange("(kc p) n -> p kc n", p=128)),
            writes=["wgs"], dma="cs_wg")
        S.add("pool", lambda e: e.dma_start(
            out=wple[:, :, :], in_=wple_d.rearrange("(kc p) n -> p kc n", p=128)),
            writes=["wple"], dma="cs_wple")

        def xslot(t, b):
            return (4 * t + b) % 6

        def load_x(t, b):
            r0 = t * TT + b * 128
            xs_ = xslot(t, b)
            S.add("sp", lambda e, r0=r0, xs_=xs_: e.dma_start(out=xb[:, xs_, :], in_=x_d[r0:r0 + 128, :]),
                  writes=[("x", xs_)], dma=("xl", xs_))

        def load_p(t):
            r0 = t * TT
            S.add("pool", lambda e, r0=r0: e.dma_start(
                out=pbf[:, :, :], in_=p_d[r0:r0 + TT, :].rearrange("(b p) c -> p b c", p=128)),
                writes=["pbf"], dma="pl")

        def load_rot(t):
            tp = t % (SEQ // TT)
            S.add("sp", lambda e, tp=tp: e.dma_start(out=rot[:, :, :], in_=ROT_d[tp]),
                  writes=["rot"], dma="rl")

        for b in range(NB):
            load_x(0, b)
        load_rot(0)
        load_p(0)
        for n in range(NRING - 1):
            issue_unit(n)
        PRO = {"pending": True}

        MAGIC = 0x5F3759DF
        CDEC_H = [float(np.exp(128.0 * np.log(1.0 - 2.0 ** (-5.0 - h)))) for h in range(4)]

        def rsqrt_dve(a_ap, y_ap, t_ap, k, keys):
            S.add("act", lambda e: e.activation(out=y_ap, in_=a_ap, func=AF.Sqrt), reads=keys, writes=keys)
            S.add("dve", lambda e: e.reciprocal(out=y_ap, in_=y_ap), reads=keys, writes=keys)

        def rstd_from_ss(ss_ap, rs_ap, a_ap, t_ap, k, n, rkeys, wkeys):
            S.add("dve", lambda e: e.tensor_scalar(out=a_ap, in0=ss_ap, scalar1=1.0 / n, scalar2=EPS,
                                                   op0=ALU.mult, op1=ALU.add), reads=rkeys, writes=wkeys)
            rsqrt_dve(a_ap, rs_ap, t_ap, k, wkeys)

        def norm_elem(b, xs_):
            S.add("act", lambda e: e.activation(out=xsb[:, b, :], in_=xb[:, xs_, :], func=AF.Square,
                                                accum_out=stats[:, b:b + 1]),
                  reads=[("x", xs_)], writes=[("xs", b), ("ss", b)])
            rstd_from_ss(stats[:, b:b + 1], stats[:, 4 + b:5 + b], stats[:, 32 + b:33 + b],
                         stats[:, 36 + b:37 + b], 1, float(D), [("ss", b)], [("rs", b)])
            S.add("act", lambda e: e.activation(out=xsb[:, b, :], in_=xb[:, xs_, :], func=AF.Identity,
                                                scale=stats[:, 4 + b:5 + b]),
                  reads=[("x", xs_), ("rs", b)], writes=[("xs", b)])

        def norm_transpose(b, gcol):
            tb = bankT()
            for kc in range(8):
                S.add("pe", lambda e, kc=kc: e.transpose(out=psT[:, tb, kc * 128:(kc + 1) * 128],
                                                         in_=xsb[:, b, kc * 128:(kc + 1) * 128],
                                                         identity=ident[:, :]),
                      reads=[("xs", b), "ident"], writes=[("psT", tb)])
            S.add("dve", lambda e: e.tensor_tensor(
                out=actT[:, :, b * 128:(b + 1) * 128],
                in0=psT[:, tb, :].rearrange("p (k n) -> p k n", k=8),
                in1=gT[:, gcol:gcol + 8, None].to_broadcast([128, 8, 128]), op=ALU.mult),
                reads=[("psT", tb), "gT"], writes=[("actT", b)])

        ALL_ACT = [("actT", b) for b in range(NB)]

        for t in range(NT):
            tp = t % (SEQ // TT)

            XS = [xslot(t, b) for b in range(NB)]
            if PRO["pending"]:
                for b in range(NB):
                    norm_elem(b, XS[b])
                PRO["pending"] = False
            if t + 1 < NT:
                load_x(t + 1, 0)
                load_x(t + 1, 1)
            def phase_B():
                for qk_i in range(2):
                    slot = next_unit()
                    for c4 in range(4):
                        c = qk_i * 4 + c4
                        bk = bankA()
                        for kc in range(8):
                            S.add("pe", lambda e, kc=kc, c4=c4, slot=slot, bk=bk: e.matmul(
                                psA[:, bk, :], ring[:, slot, kc, c4 * 128:(c4 + 1) * 128], actT[:, kc, :],
                                start=(kc == 0), stop=(kc == 7)),
                                reads=[("ring", slot)] + ALL_ACT, writes=[("psA", bk)])
                        sl = scr_slot()
                        m1 = scr[:, sl, 0:512]
                        m2 = scr[:, sl, 512:1024]
                        S.add("dve", lambda e, bk=bk, m1=m1: e.tensor_tensor(
                            out=m1, in0=psA[:, bk, :], in1=rot[:, 0, :], op=ALU.mult),
                            reads=[("psA", bk), "rot"], writes=[("scr", sl)])
                        S.add("dve", lambda e, bk=bk, m2=m2: e.tensor_tensor(
                            out=m2[0:64, :], in0=psA[64:128, bk, :], in1=rot[64:128, 1, :], op=ALU.mult),
                            reads=[("psA", bk), "rot"], writes=[("scr", sl)])
                        S.add("dve", lambda e, bk=bk, m2=m2: e.tensor_tensor(
                            out=m2[64:128, :], in0=psA[0:64, bk, :], in1=rot[0:64, 1, :], op=ALU.mult),
                            reads=[("psA", bk), "rot"], writes=[("scr", sl)])
                        S.add("pool", lambda e, c=c, m1=m1, m2=m2: e.tensor_tensor(
                            out=qk[:, c, :], in0=m1, in1=m2, op=ALU.add),
                            reads=[("scr", sl)], writes=[("qk", c)])
                if t + 1 < NT:
                    load_rot(t + 1)

            def stage_X1(b):
                pb = b % 2
                cols = slice(b * 128, (b + 1) * 128)
                banks = []
                for gi in range(4):
                    bk = bankA()
                    banks.append(bk)
                    for kc in range(8):
                        S.add("pe", lambda e, kc=kc, gi=gi, bk=bk: e.matmul(
                            psA[:, bk, :], actT[:, kc, cols], win[:, kc, gi * 512:(gi + 1) * 512],
                            start=(kc == 0), stop=(kc == 7)),
                            reads=[("actT", b), "win"], writes=[("psA", bk)])
                sl = scr_slot()
                svg = scr[:, sl, 0:512]
                S.add("act", lambda e: e.activation(out=v_bf[pb], in_=psA[:, banks[0], :], func=AF.Copy),
                      reads=[("psA", banks[0])], writes=[("v", pb)])

                def do_silu():
                    S.add("act", lambda e: e.activation(out=gsb[pb], in_=psA[:, banks[1], :], func=AF.Silu),
                          reads=[("psA", banks[1])], writes=[("gs", pb)])

                def do_gelu():
                    S.add("act", lambda e: e.activation(out=svg, in_=psA[:, banks[3], :],
                                                        func=AF.Gelu_apprx_tanh),
                          reads=[("psA", banks[3])], writes=[("scr", sl)])
                    S.add("act", lambda e: e.activation(out=ugb[pb], in_=psA[:, banks[2], :],
                                                        func=AF.Gelu_apprx_tanh),
                          reads=[("psA", banks[2])], writes=[("ug", pb)])
                if b % 2 == 0:
                    do_silu(); do_gelu()
                else:
                    do_gelu(); do_silu()
                S.add("pool", lambda e: e.tensor_tensor(out=gsb[pb], in0=gsb[pb], in1=retg[:, :], op=ALU.mult),
                      reads=[("gs", pb), "retg"], writes=[("gs", pb)])
                s6 = st6[:, pb, 0, :]
                mv = mv4[:, pb, 0, :]
                S.add("dve", lambda e: e.bn_stats(out=s6, in_=svg),
                      reads=[("scr", sl)], writes=[("st6", pb)])
                S.add("dve", lambda e: e.bn_aggr(out=mv, in_=s6), reads=[("st6", pb)], writes=[("mv", pb)])
                rs1 = stats[:, 16 + pb:17 + pb]
                a1 = stats[:, 48 + pb:49 + pb]
                S.add("dve", lambda e: e.tensor_scalar(out=a1, in0=mv4[:, pb, 0, 1:2], scalar1=EPS, scalar2=None,
                                                       op0=ALU.add),
                      reads=[("mv", pb)], writes=[("rs1", pb)])
                rsqrt_dve(a1, rs1, stats[:, 50 + pb:51 + pb], 1, [("rs1", pb)])
                S.add("dve", lambda e: e.tensor_scalar(out=svg, in0=svg, scalar1=mv4[:, pb, 0, 0:1], scalar2=rs1,
                                                       op0=ALU.subtract, op1=ALU.mult),
                      reads=[("scr", sl), ("mv", pb), ("rs1", pb)], writes=[("scr", sl)])
                S.add("pool", lambda e: e.tensor_tensor(out=svn_bf[pb], in0=svg, in1=sgg[:, :], op=ALU.mult),
                      reads=[("scr", sl), "sgg"], writes=[("svn", pb)])

            def stage_X2(b):
                pb = b % 2
                cols = slice(b * 128, (b + 1) * 128)
                last = (tp == SEQ // TT - 1 and b == NB - 1)
                if not last:
                    tb = bankT()
                    for h in range(4):
                        S.add("pe", lambda e, h=h: e.transpose(out=psT[:, tb, h * 128:(h + 1) * 128],
                                                               in_=qk[:, 4 + h, cols], identity=ident[:, :]),
                              reads=[("qk", 4 + h), "ident"], writes=[("psT", tb)])
                    S.add("dve", lambda e: e.tensor_tensor(
                        out=ktok[pb].rearrange("p (h d) -> p h d", h=4),
                        in0=psT[:, tb, 0:512].rearrange("p (h d) -> p h d", h=4),
                        in1=KD[:, :, None].to_broadcast([128, 4, 128]), op=ALU.mult),
                        reads=[("psT", tb), "KD"], writes=[("ktok", pb)])
                bk = bankA()
                for h in range(4):
                    S.add("pe", lambda e, h=h: e.matmul(psA[:, bk, h * 128:(h + 1) * 128],
                                                        qk[:, 4 + h, cols], qk[:, h, cols],
                                                        start=True, stop=True),
                          reads=[("qk", 4 + h), ("qk", h)], writes=[("psA", bk)])
                S.add("dve", lambda e: e.tensor_tensor(out=PT[pb], in0=psA[:, bk, :], in1=DTm[:, :], op=ALU.mult),
                      reads=[("psA", bk), "DTm"], writes=[("PT", pb)])

            def stage_X(b):
                stage_X1(b)
                stage_X2(b)

            def stage_Y(b):
                pb = b % 2
                cols = slice(b * 128, (b + 1) * 128)
                first = (tp == 0 and b == 0)
                last = (tp == SEQ // TT - 1 and b == NB - 1)
                gb = (t * NB + b)
                ps_in = gb % 2
                bo = bankA()
                for h in range(4):
                    hs = slice(h * 128, (h + 1) * 128)
                    S.add("pe", lambda e, hs=hs: e.matmul(psA[:, bo, hs], PT[pb][:, hs], v_bf[pb][:, hs],
                                                          start=True, stop=first),
                          reads=[("PT", pb), ("v", pb)], writes=[("psA", bo)])
                    if not first:
                        S.add("pe", lambda e, hs=hs, h=h: e.matmul(psA[:, bo, hs], qk[:, h, cols],
                                                                    Sbf[:, ps_in, hs], start=False, stop=True),
                              reads=[("qk", h), ("Sbf", ps_in)], writes=[("psA", bo)])
                if not last:
                    bu = bankA()
                    for h in range(4):
                        hs = slice(h * 128, (h + 1) * 128)
                        S.add("pe", lambda e, hs=hs: e.matmul(psA[:, bu, hs], ktok[pb][:, hs], v_bf[pb][:, hs],
                                                              start=True, stop=True),
                              reads=[("ktok", pb), ("v", pb)], writes=[("psA", bu)])
                    if first:
                        S.add("dve", lambda e: e.tensor_copy(out=Sst[:, :], in_=psA[:, bu, :]),
                              reads=[("psA", bu)], writes=["Sst"])
                    else:
                        for h in range(4):
                            hs = slice(h * 128, (h + 1) * 128)
                            S.add("dve", lambda e, hs=hs, h=h: e.scalar_tensor_tensor(
                                out=Sst[:, hs], in0=Sst[:, hs], scalar=float(CDEC_H[h]), in1=psA[:, bu, hs],
                                op0=ALU.mult, op1=ALU.add),
                                reads=["Sst", ("psA", bu)], writes=["Sst"])
                    S.add("act", lambda e: e.activation(out=Sbf[:, 1 - ps_in, :], in_=Sst[:, :], func=AF.Copy),
                          reads=["Sst"], writes=[("Sbf", 1 - ps_in)])
                for h in range(4):
                    S.add("dve", lambda e, h=h: e.bn_stats(out=st6[:, pb, h, :], in_=psA[:, bo, h * 128:(h + 1) * 128]),
                          reads=[("psA", bo)], writes=[("st6", pb)])
                for h in range(4):
                    S.add("dve", lambda e, h=h: e.bn_aggr(out=mv4[:, pb, h, :], in_=st6[:, pb, h, :]),
                          reads=[("st6", pb)], writes=[("mv", pb)])
                rs4 = stats[:, 20 + 4 * pb:24 + 4 * pb]
                a4 = stats[:, 64 + 4 * pb:68 + 4 * pb]
                S.add("dve", lambda e: e.tensor_tensor(out=a4, in0=mv4[:, pb, :, 1], in1=EPSQ[:, :], op=ALU.add),
                      reads=[("mv", pb), "EPSQ"], writes=[("rs4", pb)])
                rsqrt_dve(a4, rs4, stats[:, 72 + 4 * pb:76 + 4 * pb], 4, [("rs4", pb)])
                sl = scr_slot()
                rn = scr[:, sl, 0:512]
                for h in range(4):
                    hs = slice(h * 128, (h + 1) * 128)
                    S.add("dve", lambda e, hs=hs, h=h: e.tensor_scalar(
                        out=rn[:, hs], in0=psA[:, bo, hs], scalar1=mv4[:, pb, h, 0:1],
                        scalar2=stats[:, 20 + 4 * pb + h:21 + 4 * pb + h],
                        op0=ALU.subtract, op1=ALU.mult),
                        reads=[("psA", bo), ("mv", pb), ("rs4", pb)], writes=[("scr", sl)])
                S.add("pool", lambda e: e.tensor_tensor(out=ybf[pb][:, 0:512], in0=rn, in1=gsb[pb], op=ALU.mult),
                      reads=[("scr", sl), ("gs", pb)], writes=[("y", pb)])
                bs_ = bankA()
                for g in range(4):
                    hs = slice(g * 128, (g + 1) * 128)
                    S.add("pe", lambda e, hs=hs: e.matmul(psA[:, bs_, hs], wsTb[:, hs], svn_bf[pb][:, hs],
                                                          start=True, stop=True),
                          reads=["wsTb", ("svn", pb)], writes=[("psA", bs_)])
                tg = scr[:, sl, 512:1024]
                S.add("dve", lambda e: e.tensor_tensor(
                    out=tg.rearrange("p (g c) -> p g c", g=4),
                    in0=psA[:, bs_, :].rearrange("p (g c) -> p g c", g=4),
                    in1=bsT[:, :, None].to_broadcast([128, 4, 128]), op=ALU.add),
                    reads=[("psA", bs_), "bsT"], writes=[("scr", sl)])
                S.add("pool", lambda e: e.tensor_tensor(out=ybf[pb][:, 512:1024], in0=tg, in1=ugb[pb], op=ALU.mult),
                      reads=[("scr", sl), ("ug", pb)], writes=[("y", pb)])

            def stage_Z(b):
                pb = b % 2
                tb = bankT()
                for kc in range(8):
                    S.add("pe", lambda e, kc=kc: e.transpose(out=psT[:, tb, kc * 128:(kc + 1) * 128],
                                                             in_=ybf[pb][:, kc * 128:(kc + 1) * 128],
                                                             identity=ident[:, :]),
                          reads=[("y", pb), "ident"], writes=[("psT", tb)])
                S.add("act", lambda e: e.activation(out=actT[:, :, b * 128:(b + 1) * 128],
                                                    in_=psT[:, tb, :].rearrange("p (k n) -> p k n", k=8),
                                                    func=AF.Copy),
                      reads=[("psT", tb)], writes=[("actT", b)])

            def stage_W(b):
                cols = slice(b * 128, (b + 1) * 128)
                for half in range(2):
                    bk = bankA()
                    hsl = slice(half * 512, (half + 1) * 512)
                    for kc in range(8):
                        S.add("pe", lambda e, kc=kc, bk=bk, hsl=hsl: e.matmul(
                            psA[:, bk, :], actT[:, kc, cols], wout[:, kc, hsl],
                            start=(kc == 0), stop=(kc == 7)),
                            reads=[("actT", b), "wout"], writes=[("psA", bk)])
                    S.add("dve", lambda e, bk=bk, hsl=hsl, xs_=XS[b]: e.tensor_tensor(
                        out=xb[:, xs_, hsl], in0=xb[:, xs_, hsl], in1=psA[:, bk, :], op=ALU.add),
                        reads=[("x", XS[b]), ("psA", bk)], writes=[("x", XS[b])])
                norm_elem(b, XS[b])

            def stage_DT(b):
                norm_transpose(b, 8)

            norm_transpose(0, 0)
            norm_transpose(1, 0)
            stage_X1(0)
            stage_X1(1)
            norm_transpose(2, 0)
            norm_transpose(3, 0)
            phase_B()
            order = [("X2", 0), ("X2", 1), ("Y", 0), ("X", 2), ("Y", 1), ("Z", 0), ("X", 3), ("Y", 2), ("Z", 1),
                     ("W", 0), ("Y", 3), ("Z", 2), ("W", 1), ("DT", 0), ("Z", 3), ("W", 2), ("DT", 1), ("W", 3),
                     ("DT", 2), ("DT", 3)]
            fns = {"X": stage_X, "X2": stage_X2, "Y": stage_Y, "Z": stage_Z, "W": stage_W, "DT": stage_DT}
            for nm, b in order:
                fns[nm](b)

            for u in range(8):
                slot = next_unit()
                for f4 in range(4):
                    f = u * 4 + f4
                    bk = bankA()
                    for kc in range(8):
                        S.add("pe", lambda e, kc=kc, f4=f4, slot=slot, bk=bk: e.matmul(
                            psA[:, bk, :], ring[:, slot, kc, f4 * 128:(f4 + 1) * 128], actT[:, kc, :],
                            start=(kc == 0), stop=(kc == 7)),
                            reads=[("ring", slot)] + ALL_ACT, writes=[("psA", bk)])
                    sl = scr_slot()
                    r_ = scr[:, sl, 0:512]
                    S.add("act", lambda e, bk=bk, r_=r_: e.activation(out=r_, in_=psA[:, bk, :], func=AF.Relu),
                          reads=[("psA", bk)], writes=[("scr", sl)])
                    S.add("dve" if f % 2 == 0 else "pool",
                          lambda e, f=f, r_=r_: e.tensor_tensor(out=hT[:, f, :], in0=r_, in1=r_, op=ALU.mult),
                          reads=[("scr", sl)], writes=[("hT", f)])
                if u == 2:
                    for b in range(NB):
                        tb = bankT()
                        for kc in range(2):
                            S.add("pe", lambda e, kc=kc, b=b, tb=tb: e.transpose(
                                out=psT[:, tb, kc * 128:(kc + 1) * 128],
                                in_=pbf[:, b, kc * 128:(kc + 1) * 128], identity=ident[:, :]),
                                reads=["pbf", "ident"], writes=[("psT", tb)])
                        S.add("act", lambda e, b=b, tb=tb: e.activation(
                            out=pT[:, :, b * 128:(b + 1) * 128],
                            in_=psT[:, tb, 0:256].rearrange("p (k n) -> p k n", k=2), func=AF.Copy),
                            reads=[("psT", tb)], writes=[("pT", b)])
                    if t + 1 < NT:
                        load_p(t + 1)

            for dh in range(2):
                hsl = slice(dh * 512, (dh + 1) * 512)
                fb = [bankA() for _ in range(NB)]
                for fg in range(4):
                    slot = next_unit()
                    for b in range(NB):
                        cols = slice(b * 128, (b + 1) * 128)
                        for fc in range(8):
                            f = fg * 8 + fc
                            S.add("pe", lambda e, f=f, fc=fc, slot=slot, fbk=fb[b], cols=cols: e.matmul(
                                psA[:, fbk, :], hT[:, f, cols], ring[:, slot, fc, :],
                                start=(f == 0), stop=(f == 31)),
                                reads=[("hT", f), ("ring", slot)], writes=[("psA", fb[b])])
                for b in range(NB):
                    S.add("dve", lambda e, xs_=XS[b], hsl=hsl, fbk=fb[b]: e.tensor_tensor(
                        out=xb[:, xs_, hsl], in0=xb[:, xs_, hsl], in1=psA[:, fbk, :], op=ALU.add),
                        reads=[("x", XS[b]), ("psA", fb[b])], writes=[("x", XS[b])])

            for b in range(NB):
                norm_elem(b, XS[b])

            def stage_G(b):
                cols = slice(b * 128, (b + 1) * 128)
                for half in range(2):
                    hsl = slice(half * 512, (half + 1) * 512)
                    bg = bankA()
                    for kc in range(8):
                        S.add("pe", lambda e, kc=kc, bg=bg, hsl=hsl: e.matmul(
                            psA[:, bg, :], actT[:, kc, cols], wgs[:, kc, hsl],
                            start=(kc == 0), stop=(kc == 7)),
                            reads=[("actT", b), "wgs"], writes=[("psA", bg)])
                    bp = bankA()
                    for kc in range(2):
                        S.add("pe", lambda e, kc=kc, bp=bp, hsl=hsl: e.matmul(
                            psA[:, bp, :], pT[:, kc, cols], wple[:, kc, hsl],
                            start=(kc == 0), stop=(kc == 1)),
                            reads=[("pT", b), "wple"], writes=[("psA", bp)])
                    sl = scr_slot()
                    gt = scr[:, sl, 0:512]
                    S.add("act", lambda e, bg=bg, gt=gt: e.activation(out=gt, in_=psA[:, bg, :], func=AF.Sigmoid),
                          reads=[("psA", bg)], writes=[("scr", sl)])
                    S.add("dve", lambda e, bp=bp, gt=gt: e.tensor_tensor(out=gt, in0=gt, in1=psA[:, bp, :],
                                                                         op=ALU.mult),
                          reads=[("scr", sl), ("psA", bp)], writes=[("scr", sl)])
                    S.add("pool", lambda e, gt=gt, hsl=hsl, xs_=XS[b]: e.tensor_tensor(out=xb[:, xs_, hsl],
                                                                                        in0=xb[:, xs_, hsl],
                                                                                        in1=gt, op=ALU.add),
                          reads=[("x", XS[b]), ("scr", sl)], writes=[("x", XS[b])])

            def stage_H(b):
                ss = stats[:, 8 + b:9 + b]
                rs = stats[:, 12 + b:13 + b]
                sl = scr_slot()
                junk = scr[:, sl, :].bitcast(BF16)[:, 0:1024]
                S.add("act", lambda e, xs_=XS[b]: e.activation(out=junk, in_=xb[:, xs_, :], func=AF.Square,
                                                               accum_out=ss),
                      reads=[("x", XS[b])], writes=[("scr", sl), ("ssH", b)])
                rstd_from_ss(ss, rs, stats[:, 40 + b:41 + b], stats[:, 44 + b:45 + b], 1, float(D),
                             [("ssH", b)], [("rsH", b)])
                sl2 = scr_slot()
                S.add("dve", lambda e, xs_=XS[b]: e.scalar_tensor_tensor(out=scr[:, sl2, :], in0=xb[:, xs_, :], scalar=rs,
                                                              in1=gfin[:, :], op0=ALU.mult, op1=ALU.mult),
                      reads=[("x", XS[b]), ("rsH", b), "gfin"], writes=[("scr", sl2)])
                r0 = t * TT + b * 128
                S.add("sp", lambda e: e.dma_start(out=out_d[r0:r0 + 128, :], in_=scr[:, sl2, :]),
                      reads=[("scr", sl2)], dma=("st", sl2))
                if t + 1 < NT and b < 2:
                    load_x(t + 1, b + 2)

            gorder = [("T", 0), ("T", 1), ("G", 0), ("T", 2), ("G", 1), ("H", 0), ("T", 3), ("G", 2), ("A", 0),
                      ("H", 1), ("G", 3), ("A", 1), ("H", 2), ("A", 2), ("H", 3), ("A", 3)]
            for nm, b in gorder:
                if nm == "T":
                    norm_transpose(b, 16)
                elif nm == "G":
                    stage_G(b)
                elif nm == "H":
                    stage_H(b)
                elif t + 1 < NT:
                    norm_elem(b, xslot(t + 1, b))

        S.finalize()
        store_keys = [k for k in S.dma_cum if isinstance(k, tuple) and k[0] == "st"]

        sems = {}
        for e in ("pe", "act", "dve", "pool"):
            sems[e] = es.enter_context(nc.semaphore("s_" + e))
        for i, k in enumerate(S.dma_cum):
            sems[("dma", k)] = es.enter_context(nc.semaphore("d%d" % i))
        for k, v in S.dma_cum.items():
            assert v < 60000, (k, v)

        with nc.Block() as block:
            @block.tensor
            def _(eobj):
                _emit_engine("pe", eobj, S, sems)

            @block.scalar
            def _(eobj):
                _emit_engine("act", eobj, S, sems)

            @block.vector
            def _(eobj):
                _emit_engine("dve", eobj, S, sems)

            @block.gpsimd
            def _(eobj):
                _emit_engine("pool", eobj, S, sems)

            @block.sync
            def _(eobj):
                _emit_engine("sp", eobj, S, sems)
                for k in store_keys:
                    eobj.wait_ge(sems[("dma", k)], S.dma_cum[k])
    return nc


def make_in_maps(inputs, n_cores, nseq):
    f32 = np.float32
    c = _host_consts()
    x = np.asarray(inputs["x"], f32)
    p = np.asarray(inputs["p"], f32)[0]
    g3 = np.stack([np.asarray(inputs[k], f32)[0].reshape(8, 128).T for k in ("g_mix", "g_ffn", "g_ple")], axis=1)
    gT = np.ascontiguousarray(g3.reshape(128, 24))
    shared = {
        "w_in": np.ascontiguousarray(np.asarray(inputs["w_in"], f32)[0]),
        "w_out": np.ascontiguousarray(np.asarray(inputs["w_out"], f32)[0]),
        "w_ff1": np.ascontiguousarray(np.asarray(inputs["w_ff1"], f32)[0]),
        "w_ff2": np.ascontiguousarray(np.asarray(inputs["w_ff2"], f32)[0]),
        "w_gate": np.ascontiguousarray(np.asarray(inputs["w_ple_gate"], f32)[0]),
        "w_ple": np.ascontiguousarray(np.asarray(inputs["w_ple"], f32)[0]),
        "gT": gT,
        "gfin_b": np.ascontiguousarray(np.broadcast_to(np.asarray(inputs["g_final"], f32)[None, :], (128, D))),
        "retg_b": np.ascontiguousarray(np.broadcast_to(np.asarray(inputs["ret_norm_g"], f32)[0][None, :], (128, 512))),
        "sgg_b": np.ascontiguousarray(np.broadcast_to(np.asarray(inputs["sg_norm_g"], f32)[0][None, :], (128, 512))),
        "bsT": np.ascontiguousarray(np.asarray(inputs["b_s"], f32)[0].T),
        "wsT": np.ascontiguousarray(np.asarray(inputs["w_s"], f32)[0].transpose(2, 0, 1).reshape(128, 512)),
        "DTm": c["DTm"], "KD": c["KD"], "EPSQ": c["EPSQ"], "MASKT": c["MASKT"],
        "ROT": c["ROT"], "IDENT": c["IDENT"],
    }
    in_maps = []
    for ci in range(n_cores):
        b0 = ci * nseq
        m = dict(shared)
        m["x"] = np.ascontiguousarray(x[b0:b0 + nseq].reshape(nseq * SEQ, D))
        m["p"] = np.ascontiguousarray(p[b0:b0 + nseq].reshape(nseq * SEQ, 256))
        in_maps.append(m)
    return in_maps


def kernel(x, p, g_mix, w_in, ret_norm_g, sg_norm_g, w_s, b_s, w_out, g_ffn,
           w_ff1, w_ff2, g_ple, w_ple_gate, w_ple, g_final):
    inputs = dict(x=x, p=p, g_mix=g_mix, w_in=w_in, ret_norm_g=ret_norm_g, sg_norm_g=sg_norm_g, w_s=w_s,
                  b_s=b_s, w_out=w_out, g_ffn=g_ffn, w_ff1=w_ff1, w_ff2=w_ff2, g_ple=g_ple,
                  w_ple_gate=w_ple_gate, w_ple=w_ple, g_final=g_final)
    nc = build(SEQ_PER_CORE)
    in_maps = make_in_maps(inputs, N_CORES, SEQ_PER_CORE)
    res = run_bass_kernel_spmd(nc, in_maps, core_ids=list(range(N_CORES)))
    outs = [np.asarray(r["out"], np.float32).reshape(SEQ_PER_CORE, SEQ, D) for r in res.results]
    return np.concatenate(outs, axis=0)
```
